# Optimizing a Trainium2 kernel written in Bass

```python
import numpy as np
import jax
import jax.numpy as jnp
from jax import lax

D_MODEL = 2048
BATCH = 4
SEQ = 4096
DEPTH = 1

D_MIX = D_MODEL
NSA_HEADS = 16
NSA_KV_GROUPS = 4
NSA_REP = NSA_HEADS // NSA_KV_GROUPS
HEAD_DIM = 64
NSA_WIDTH = NSA_HEADS * HEAD_DIM
GMLP_WIDTH = D_MIX - NSA_WIDTH
GMLP_GROUPS = 8
GMLP_GROUP_DIM = GMLP_WIDTH // GMLP_GROUPS
GMLP_CHUNK = 128
CMP_BLOCK = 32
CMP_STRIDE = 16
CMP_HIDDEN = 256
SEL_BLOCK = 64
SEL_TOPK = 16
WINDOW = 512
Q_BLOCK = 128
N_BRANCH = 3
Q_COLS = NSA_WIDTH
KV_COLS = NSA_KV_GROUPS * HEAD_DIM
GATE_COLS = NSA_HEADS * N_BRANCH
IN_COLS = Q_COLS + 6 * KV_COLS + GATE_COLS + 2 * GMLP_WIDTH
D_FF = 5632
CONV_WIDTH = 3
NORM_EPS = 1e-6
LN_EPS = 1e-5
NEG_BIG = -1e30
SEL_BIG = 1e9

kernel_name = "hybrid_nsa_gmlp_convffn"


def _rms_norm(x, w):
    xf = x.astype(jnp.float32)
    y = xf * lax.rsqrt(jnp.mean(xf * xf, axis=-1, keepdims=True) + NORM_EPS)
    return (y * w.astype(jnp.float32)).astype(x.dtype)


def _layer_norm(x, w, b):
    xf = x.astype(jnp.float32)
    mu = jnp.mean(xf, axis=-1, keepdims=True)
    var = jnp.mean(jnp.square(xf - mu), axis=-1, keepdims=True)
    y = (xf - mu) * lax.rsqrt(var + LN_EPS)
    return (y * w.astype(jnp.float32) + b.astype(jnp.float32)).astype(x.dtype)


def _masked_softmax(s, mask):
    s = jnp.where(mask, s, NEG_BIG)
    m = jnp.max(s, axis=-1, keepdims=True)
    e = jnp.where(mask, jnp.exp(s - m), 0.0)
    return e / jnp.maximum(jnp.sum(e, axis=-1, keepdims=True), 1e-30)


def _alibi_slopes(n):
    return np.power(2.0, -8.0 * np.arange(1, n + 1) / n).astype(np.float32)


def _cmp_sel_overlap(n_cmp, n_sel):
    start = np.arange(n_cmp)[:, None] * CMP_STRIDE
    s0 = np.arange(n_sel)[None, :] * SEL_BLOCK
    return ((start < s0 + SEL_BLOCK) & (start + CMP_BLOCK > s0)).astype(np.float32)


def _compress(raw, idx, pos, w1, w2):
    blk = raw[:, idx] + pos[None, None, :, None, :]
    b, n, l, g, d = blk.shape
    blk = blk.transpose(0, 3, 1, 2, 4).reshape(b, g, n, l * d)
    return jax.nn.gelu(blk @ w1) @ w2


def _nsa(q, kc_raw, vc_raw, ks, vs, kw, vw, gate_logits, q_norm_w, k_norm_w, cmp_pos, cmp_w1, cmp_w2):
    B, T, _ = q.shape
    G, R, Dh = NSA_KV_GROUPS, NSA_REP, HEAD_DIM
    n_cmp = (T - CMP_BLOCK) // CMP_STRIDE + 1
    n_sel = T // SEL_BLOCK
    k_sel = min(SEL_TOPK, n_sel)
    n_qb = T // Q_BLOCK

    q = _rms_norm(q.reshape(B, T, NSA_HEADS, Dh), q_norm_w) * (Dh ** -0.5)
    q = q.reshape(B, T, G, R, Dh).transpose(0, 2, 3, 1, 4)
    gates = jax.nn.sigmoid(gate_logits.astype(jnp.float32))
    gates = gates.reshape(B, T, G, R, N_BRANCH).transpose(0, 2, 3, 1, 4)

    def kv_heads(a):
        return a.reshape(B, T, G, Dh)

    idx_c = np.arange(n_cmp)[:, None] * CMP_STRIDE + np.arange(CMP_BLOCK)[None, :]
    kc = _rms_norm(_compress(kv_heads(kc_raw), idx_c, cmp_pos[0], cmp_w1[0], cmp_w2[0]), k_norm_w[0])
    vc = _compress(kv_heads(vc_raw), idx_c, cmp_pos[1], cmp_w1[1], cmp_w2[1])
    cmp_end = jnp.asarray(idx_c[:, -1].astype(np.int32))
    cmp_center = jnp.asarray(idx_c.mean(axis=1).astype(np.float32))
    overlap = jnp.asarray(_cmp_sel_overlap(n_cmp, n_sel))

    ks = _rms_norm(kv_heads(ks), k_norm_w[1]).transpose(0, 2, 1, 3).reshape(B, G, n_sel, SEL_BLOCK, Dh)
    vs = kv_heads(vs).transpose(0, 2, 1, 3).reshape(B, G, n_sel, SEL_BLOCK, Dh)

    pad = ((0, 0), (0, 0), (WINDOW, 0), (0, 0))
    kw = jnp.pad(_rms_norm(kv_heads(kw), k_norm_w[2]).transpose(0, 2, 1, 3), pad)
    vw = jnp.pad(kv_heads(vw).transpose(0, 2, 1, 3), pad)

    slopes = jnp.asarray(_alibi_slopes(NSA_HEADS).reshape(G, R))[None, :, :, None, None]
    b_idx = jnp.arange(B)[:, None, None, None]
    g_idx = jnp.arange(G)[None, :, None, None]
    sel_off = jnp.arange(SEL_BLOCK, dtype=jnp.int32)
    win_off = jnp.arange(WINDOW + Q_BLOCK, dtype=jnp.int32)
    blk_ids = jnp.arange(n_sel, dtype=jnp.int32)

    def attend_block(qb):
        q0 = qb * Q_BLOCK
        qblk = lax.dynamic_slice_in_dim(q, q0, Q_BLOCK, axis=3)
        gblk = lax.dynamic_slice_in_dim(gates, q0, Q_BLOCK, axis=3)
        t = q0 + jnp.arange(Q_BLOCK, dtype=jnp.int32)
        tf = t.astype(jnp.float32)

        s_c = jnp.einsum('bgrqd,bgnd->bgrqn', qblk, kc).astype(jnp.float32)
        s_c = s_c - slopes * (tf[:, None] - cmp_center[None, :])
        p_c = _masked_softmax(s_c, cmp_end[None, :] <= t[:, None])
        o_c = jnp.einsum('bgrqn,bgnd->bgrqd', p_c.astype(vc.dtype), vc)

        imp = jnp.einsum('bgrqn,nj->bgqj', p_c, overlap)
        cur = t[:, None] // SEL_BLOCK
        valid = blk_ids[None, :] <= cur
        forced = (blk_ids[None, :] == 0) | (blk_ids[None, :] == cur) | (blk_ids[None, :] == cur - 1)
        score = jnp.where(valid, jnp.where(forced, SEL_BIG, imp), -SEL_BIG)
        top_val, top_idx = lax.top_k(score, k_sel)
        kg = ks[b_idx, g_idx, top_idx]
        vg = vs[b_idx, g_idx, top_idx]
        pos = top_idx[..., None] * SEL_BLOCK + sel_off
        m_s = (top_val > -0.5 * SEL_BIG)[..., None] & (pos <= t[:, None, None])
        s_s = jnp.einsum('bgrqd,bgqkld->bgrqkl', qblk, kg).astype(jnp.float32)
        s_s = s_s - slopes[..., None] * (tf[:, None, None] - pos.astype(jnp.float32))[:, :, None]
        n_keys = k_sel * SEL_BLOCK
        p_s = _masked_softmax(s_s.reshape(B, G, R, Q_BLOCK, n_keys),
                              m_s.reshape(B, G, 1, Q_BLOCK, n_keys)).reshape(s_s.shape)
        o_s = jnp.einsum('bgrqkl,bgqkld->bgrqd', p_s.astype(vg.dtype), vg)

        kwb = lax.dynamic_slice_in_dim(kw, q0, WINDOW + Q_BLOCK, axis=2)
        vwb = lax.dynamic_slice_in_dim(vw, q0, WINDOW + Q_BLOCK, axis=2)
        posw = q0 - WINDOW + win_off
        dist = t[:, None] - posw[None, :]
        m_w = (posw[None, :] >= 0) & (dist >= 0) & (dist < WINDOW)
        s_w = jnp.einsum('bgrqd,bgkd->bgrqk', qblk, kwb).astype(jnp.float32)
        s_w = s_w - slopes * dist.astype(jnp.float32)
        p_w = _masked_softmax(s_w, m_w)
        o_w = jnp.einsum('bgrqk,bgkd->bgrqd', p_w.astype(vwb.dtype), vwb)

        o = gblk[..., 0:1] * o_c + gblk[..., 1:2] * o_s + gblk[..., 2:3] * o_w
        return o.astype(q.dtype)

    out = lax.map(attend_block, jnp.arange(n_qb, dtype=jnp.int32))
    return out.transpose(1, 0, 4, 2, 3, 5).reshape(B, T, NSA_WIDTH)


def _spatial_gating(z, ln_w, ln_b, sw, sb):
    B, T, _ = z.shape
    u, v = jnp.split(jax.nn.gelu(z), 2, axis=-1)
    v = _layer_norm(v, ln_w, ln_b)
    v = v.reshape(B, T // GMLP_CHUNK, GMLP_CHUNK, GMLP_GROUPS, GMLP_GROUP_DIM)
    causal = jnp.tril(jnp.ones((GMLP_CHUNK, GMLP_CHUNK), dtype=bool))
    w = jnp.where(causal[None], sw, jnp.zeros_like(sw))
    v_mix = jnp.einsum('gts,bnsgc->bntgc', w, v) + sb.T[None, None, :, :, None]
    return u * v_mix.reshape(B, T, GMLP_WIDTH)


def _conv_ffn(x, w_up, conv_w, conv_b, w_down):
    h = x @ w_up
    T = h.shape[1]
    hp = jnp.pad(h, ((0, 0), (CONV_WIDTH - 1, 0), (0, 0)))
    h = conv_b + sum(conv_w[k] * hp[:, k:k + T] for k in range(CONV_WIDTH))
    gate, up = jnp.split(h, 2, axis=-1)
    return (jax.nn.silu(gate) * up) @ w_down


def setup_inputs(seed: int = 0) -> dict:
    key = jax.random.key(seed)
    ks = jax.random.split(key, 18)
    f32 = jnp.float32
    nrm = lambda k, shape, s: jax.random.normal(k, shape, f32) * s
    return {
        "x": nrm(ks[0], (BATCH, SEQ, D_MODEL), 1.0),
        "attn_norm_w": 1.0 + nrm(ks[1], (D_MODEL,), 0.02),
        "w_in": nrm(ks[2], (D_MODEL, IN_COLS), D_MODEL ** -0.5),
        "q_norm_w": 1.0 + nrm(ks[3], (HEAD_DIM,), 0.02),
        "k_norm_w": 1.0 + nrm(ks[4], (N_BRANCH, HEAD_DIM), 0.02),
        "cmp_pos": nrm(ks[5], (2, CMP_BLOCK, HEAD_DIM), 0.1),
        "cmp_w1": nrm(ks[6], (2, CMP_BLOCK * HEAD_DIM, CMP_HIDDEN), (CMP_BLOCK * HEAD_DIM) ** -0.5),
        "cmp_w2": nrm(ks[7], (2, CMP_HIDDEN, HEAD_DIM), CMP_HIDDEN ** -0.5),
        "gmlp_ln_w": 1.0 + nrm(ks[8], (GMLP_WIDTH,), 0.02),
        "gmlp_ln_b": nrm(ks[9], (GMLP_WIDTH,), 0.02),
        "spatial_w": nrm(ks[10], (GMLP_GROUPS, GMLP_CHUNK, GMLP_CHUNK), 0.5 * GMLP_CHUNK ** -0.5),
        "spatial_b": 1.0 + nrm(ks[11], (GMLP_GROUPS, GMLP_CHUNK), 0.1),
        "w_out": nrm(ks[12], (D_MIX, D_MODEL), D_MIX ** -0.5),
        "ffn_norm_w": 1.0 + nrm(ks[13], (D_MODEL,), 0.02),
        "w_up": nrm(ks[14], (D_MODEL, 2 * D_FF), D_MODEL ** -0.5),
        "conv_w": nrm(ks[15], (CONV_WIDTH, 2 * D_FF), CONV_WIDTH ** -0.5),
        "conv_b": nrm(ks[16], (2 * D_FF,), 0.02),
        "w_down": nrm(ks[17], (D_FF, D_MODEL), D_FF ** -0.5),
    }


def reference(x, attn_norm_w, w_in, q_norm_w, k_norm_w, cmp_pos, cmp_w1, cmp_w2, gmlp_ln_w, gmlp_ln_b,
              spatial_w, spatial_b, w_out, ffn_norm_w, w_up, conv_w, conv_b, w_down):
    splits = [int(c) for c in np.cumsum([Q_COLS] + [KV_COLS] * 6 + [GATE_COLS])]
    for _ in range(DEPTH):
        h = _rms_norm(x, attn_norm_w)
        proj = h @ w_in
        q, kc, vc, ksl, vsl, kwn, vwn, gl, z = jnp.split(proj, splits, axis=-1)
        a = _nsa(q, kc, vc, ksl, vsl, kwn, vwn, gl, q_norm_w, k_norm_w, cmp_pos, cmp_w1, cmp_w2)
        b = _spatial_gating(z, gmlp_ln_w, gmlp_ln_b, spatial_w, spatial_b)
        x = x + jnp.concatenate([a, b], axis=-1) @ w_out
        x = x + _conv_ffn(_rms_norm(x, ffn_norm_w), w_up, conv_w, conv_b, w_down)
    return x
```

```python
import numpy as np
import ml_dtypes
import concourse.bass as bass
import concourse.mybir as mybir
from concourse.ap import AP
from concourse.bass_utils import run_bass_kernel_spmd

F32 = mybir.dt.float32
BF16 = mybir.dt.bfloat16
AF = mybir.ActivationFunctionType
ALU = mybir.AluOpType
AX = mybir.AxisListType


class Prog:
    ENGS = ('pe', 'act', 'dve', 'pool', 'sp')

    def __init__(self, nc, n_dma_sems=48):
        self.nc = nc
        self.q = {e: [] for e in self.ENGS}
        self.sem = {e: nc.alloc_semaphore("sem_" + e) for e in self.ENGS}
        self.cnt = {e: 0 for e in self.ENGS}
        self.seen = {e: {} for e in self.ENGS}
        self.dsems = {'sp': [[nc.alloc_semaphore("dsp%d" % i), 0] for i in range(n_dma_sems)],
                      'pool': [[nc.alloc_semaphore("dpl%d" % i), 0] for i in range(28)],
                      'act': [[nc.alloc_semaphore("dac%d" % i), 0] for i in range(4)]}
        self.dnext = {'sp': 0, 'pool': 0, 'act': 0}
        self.last_w = {}
        self.readers = {}
        self.finals = []
        self.semobj = {}

    def _deps(self, eng, reads, writes):
        waits = {}

        def need(ev):
            if ev is None:
                return
            sid, val, src = ev
            if eng == 'pe' and src == 'pe':
                return
            if self.seen[eng].get(sid, 0) >= val:
                return
            if waits.get(sid, 0) < val:
                waits[sid] = val

        for k in reads:
            need(self.last_w.get(k))
        for k in writes:
            need(self.last_w.get(k))
            for sid, (val, src) in self.readers.get(k, {}).items():
                need((sid, val, src))
        return waits

    def _commit(self, eng, waits, fn, ev, inc, reads, writes):
        for sid, val in waits.items():
            self.seen[eng][sid] = val
        self.q[eng].append((list(waits.items()), fn, ev, inc))
        for k in reads:
            d = self.readers.setdefault(k, {})
            if d.get(ev[0], (0, None))[0] < ev[1]:
                d[ev[0]] = (ev[1], ev[2])
        for k in writes:
            self.last_w[k] = ev
            self.readers[k] = {}

    def op(self, eng, fn, reads=(), writes=()):
        waits = self._deps(eng, reads, writes)
        self.cnt[eng] += 1
        sem = self.sem[eng]
        self.semobj[id(sem)] = sem
        ev = (id(sem), self.cnt[eng], eng)
        self._commit(eng, waits, fn, ev, 1, reads, writes)

    def dma(self, eng, out, in_, reads=(), writes=(), final=False, pace=(), **kw):
        waits = self._deps(eng, list(reads) + list(pace), writes)
        pool_ = self.dsems[eng]
        slot = pool_[self.dnext[eng]]
        self.dnext[eng] = (self.dnext[eng] + 1) % len(pool_)
        sem = slot[0]
        self.semobj[id(sem)] = sem
        if slot[1] > 0 and self.seen[eng].get(id(sem), 0) < slot[1]:
            waits[id(sem)] = max(waits.get(id(sem), 0), slot[1])
        slot[1] += 16
        ev = (id(sem), slot[1], 'dma')
        self._commit(eng, waits, lambda e: e.dma_start(out=out, in_=in_, **kw), ev, 16, reads, writes)
        if final:
            self.finals.append(ev)

    def barrier(self):
        targets = {}
        for e in self.ENGS:
            if self.cnt[e] > 0:
                targets[id(self.sem[e])] = self.cnt[e]
        for sem, val in [x for lst in self.dsems.values() for x in lst]:
            if val > 0:
                targets[id(sem)] = val
                self.semobj[id(sem)] = sem
        for e in self.ENGS:
            waits = []
            for sid, val in targets.items():
                if sid == id(self.sem[e]) and e == 'pe':
                    pass
                if self.seen[e].get(sid, 0) < val:
                    waits.append((sid, val))
                    self.seen[e][sid] = val
            self.q[e].append((waits, None, None, 0))
        self.last_w = {}
        self.readers = {}

    def emit(self):
        nc = self.nc
        q = self.q
        semobj = self.semobj

        def run(e, lst):
            for waits, fn, ev, inc in lst:
                for sid, val in waits:
                    e.wait_ge(semobj[sid], val)
                if fn is not None:
                    ins = fn(e)
                    ins.then_inc(semobj[ev[0]], inc)

        with nc.Block() as block:
            @block.sync
            def _(e):
                run(e, q['sp'])

            @block.tensor
            def _(e):
                run(e, q['pe'])

            @block.scalar
            def _(e):
                run(e, q['act'])

            @block.vector
            def _(e):
                run(e, q['dve'])

            @block.gpsimd
            def _(e):
                run(e, q['pool'])
        self.q = {e: [] for e in self.ENGS}

    def finish(self):
        fw = [(sid, val) for sid, val, _ in self.finals]
        self.q['sp'].append((fw, None, None, 0))
        self.emit()

D = 2048
KC = 16
TV = 4096
TT = 256
NTILE = TV // TT
OWN0 = 2048
IN_COLS = 4656
DFF = 5632
NFC = 44
NEG = -30000.0
C_Q, C_KC, C_VC, C_KS, C_VS, C_KW, C_VW, C_GL, C_U, C_V = 0, 1024, 1280, 1536, 1792, 2048, 2304, 2560, 2608, 3632
GELU_K = 1.5957691216057308


def tap(t, p0, npart, off, dims):
    ps = t[:].ap[0][0]
    return AP(t, p0 * ps + off, [[ps, npart]] + [list(d) for d in dims])


def build_program(stop_after_mixer=False, dbg=False):
    nc = bass.Bass("TRN2", target_bir_lowering=False)

    def din(name, shape, dt=F32):
        return nc.dram_tensor(name, list(shape), dt, kind="ExternalInput").ap()

    xv = din("xv", [TV, D])
    w_in = din("w_in", [D, IN_COLS])
    w_out = din("w_out", [D, D])
    w_up = din("w_up", [D, 2 * DFF])
    w_down = din("w_down", [DFF, D])
    nw1_d = din("nw1", [128, KC])
    nw2_d = din("nw2", [128, KC])
    qw_d = din("qw", [64, 1])
    kw_d = din("kw", [64, 3])
    cpos_d = din("cpos", [128, 2, 16])
    cw1_d = din("cw1", [2, 2048, 256])
    cw2_d = din("cw2", [2, 256, 64])
    lnw_d = din("lnw", [128, 8])
    lnb_d = din("lnb", [128, 8])
    swT_d = din("swT", [128, 8, 128])
    sbb_d = din("sbb", [128, 8, 128])
    cw_d = din("cw", [128, 3, 88])
    cb_d = din("cb", [128, 88])
    ident_d = din("ident", [128, 128], BF16)
    qaug_d = din("qaug", [6, 16, TV], BF16)
    kaugs_d = din("kaugs", [6, TV], BF16)
    kaugc_d = din("kaugc", [6, 256], BF16)
    maskc_d = din("maskc", [128, 19, 128], BF16)
    cmask_d = din("cmask", [128, 2, 128], BF16)
    EB_d = din("EB", [96, TV], BF16)
    OVL_d = din("OVL", [128, 2, 65], BF16)
    fb_d = din("fb", [128, 17, 64], BF16)
    vval_d = din("vval", [128, 32], BF16)
    vcval_d = din("vcval", [128, 2])
    caus_d = din("caus", [128, 128])
    ones_d = din("ones64", [96, 64], BF16)

    out_d = nc.dram_tensor("out", [2048, D], F32, kind="ExternalOutput").ap()
    win_g = nc.dram_tensor("win_g", [19, 128, KC * 256], BF16).ap()
    wout_g = nc.dram_tensor("wout_g", [8, 128, KC * 256], BF16).ap()
    wup_g = nc.dram_tensor("wup_g", [2, 22, 128, KC * 256], BF16).ap()
    wdn_g = nc.dram_tensor("wdn_g", [4, 4, 128, 11 * 512], BF16).ap()
    x1h_d = nc.dram_tensor("x1h", [128, D], F32).ap()
    dbg_d = None
    if dbg:
        dbg_d = nc.dram_tensor("dbg", [128, 17, 2048], F32, kind="ExternalOutput").ap()

    P = Prog(nc)

    psA = [nc.alloc_psum_tensor("psA%d" % i, [128, 512], F32) for i in range(2)]
    psB = [nc.alloc_psum_tensor("psB%d" % i, [128, 512], F32) for i in range(2)]
    psC = nc.alloc_psum_tensor("psC", [128, 512], F32)
    psI = nc.alloc_psum_tensor("psI", [128, 512], F32)
    psS = nc.alloc_psum_tensor("psS", [128, 512], F32)
    psW = nc.alloc_psum_tensor("psW", [128, 512], F32)
    cntA = [0]
    cntB = [0]
    scnt = [0]

    def nextA():
        cntA[0] += 1
        i = cntA[0] % 2
        return psA[i], 'A%d' % i

    def nextB():
        cntB[0] += 1
        i = cntB[0] % 2
        return psB[i], 'B%d' % i

    win_groups = [(C_KC, 256), (C_VC, 256), (C_KS, 256), (C_KW, 256), (C_VS, 256), (C_VW, 256)]
    win_groups += [(C_Q + j * 256, 256) for j in range(4)] + [(C_GL, 48)]
    win_groups += [(C_U + j * 256, 256) for j in range(4)] + [(C_V + j * 256, 256) for j in range(4)]
    win_gid = {c0_: gi for gi, (c0_, n_) in enumerate(win_groups)}
    assert len(win_groups) == 19

    def win_src(c0_, n_):
        return win_g[win_gid[c0_]].rearrange("p (k c) -> p k c", c=256)[:, :, 0:n_]

    def wout_src(j):
        return wout_g[j].rearrange("p (k c) -> p k c", c=256)

    deferred_casts = []
    for gi_, (c0_, n_) in enumerate(win_groups):
        item_ = (win_src(c0_, n_), w_in[:, c0_:c0_ + n_].rearrange("(k p) c -> p k c", p=128), ('win', c0_))
        if gi_ < 6:
            P.dma('pool', item_[0], item_[1], writes=[item_[2]])
        else:
            deferred_casts.append(item_)
    for j in range(8):
        deferred_casts.append((wout_src(j), w_out[:, j * 256:(j + 1) * 256].rearrange("(k p) c -> p k c", p=128), ('wout', j)))
    for cg_ in range(22):
        for hf_ in range(2):
            c0_ = hf_ * DFF + cg_ * 256
            deferred_casts.append((wup_g[hf_, cg_].rearrange("p (k c) -> p k c", c=256),
                                   w_up[:, c0_:c0_ + 256].rearrange("(k p) c -> p k c", p=128), ('wup', hf_, cg_)))
    for ng_ in range(4):
        for s_ in range(4):
            deferred_casts.append((wdn_g[ng_, s_].rearrange("p (k c) -> p k c", c=512),
                                   w_down[s_ * 1408:(s_ + 1) * 1408, ng_ * 512:(ng_ + 1) * 512].rearrange("(k p) c -> p k c", p=128),
                                   ('wdn', ng_, s_)))

    def issue_casts(n, pace=()):
        for _ in range(n):
            if deferred_casts:
                o_, i_, k_ = deferred_casts.pop(0)
                P.dma('pool', o_, i_, writes=[k_], pace=pace)

    def mm_group(out, pairs, reads, writes):
        n = len(pairs)
        for i, (l, r) in enumerate(pairs):
            P.op('pe', lambda e, l=l, r=r, i=i: e.matmul(out, l, r, start=(i == 0), stop=(i == n - 1)),
                 reads=reads, writes=writes)

    def stageA1(halves, xb, ss, rs, tag):
        for i, (hap, hkey, c0, n) in enumerate(halves):
            P.op('act', lambda e, hap=hap, c0=c0, n=n, i=i: e.activation(xb[:, c0:c0 + n], hap, AF.Square, accum_out=ss[:, i:i + 1]),
                 reads=[hkey], writes=['xb' + tag, 'ss' + tag])
        if len(halves) == 2:
            P.op('dve', lambda e: e.tensor_tensor(ss[:, 0:1], ss[:, 0:1], ss[:, 1:2], ALU.add), reads=['ss' + tag], writes=['ss' + tag])
        P.op('act', lambda e: e.activation(rs[:], ss[:, 0:1], AF.Sqrt, scale=1.0 / D, bias=1e-6),
             reads=['ss' + tag], writes=['rs' + tag])
        P.op('dve', lambda e: e.reciprocal(rs[:], rs[:]), reads=['rs' + tag], writes=['rs' + tag])
        for hap, hkey, c0, n in halves:
            P.op('dve', lambda e, hap=hap, c0=c0, n=n: e.tensor_scalar(xb[:, c0:c0 + n], hap, rs[:, 0:1], None, ALU.mult),
                 reads=[hkey, 'rs' + tag], writes=['xb' + tag])

    def stageA2(xb, nw, ident, dstT, dcol0, dkey, tag):
        for h in range(2):
            ps, pk = nextA()
            psb = ps[:].bitcast(BF16)
            for k in range(8):
                kc = h * 8 + k
                P.op('pe', lambda e, kc=kc, k=k, psb=psb: e.transpose(psb[:, k * 128:(k + 1) * 128],
                                                                       xb[:, kc * 128:(kc + 1) * 128], ident[:]),
                     reads=['xb' + tag, 'ident'], writes=[pk])
            src = psb.rearrange("p (a b) -> p a b", b=128)
            nwb = tap(nw, 0, 128, h * 8, [[1, 8], [0, 128]])
            dst = dstT[:, h * 8:(h + 1) * 8, dcol0:dcol0 + 128]
            P.op('dve', lambda e, dst=dst, src=src, nwb=nwb: e.tensor_tensor(dst, src, nwb, ALU.mult),
                 reads=[pk, 'nw'], writes=[dkey])

    def rmsnorm_T(xblk, xkey, xb, junk, ss, rs, nw, ident, dstT, dcol0, dkey, tag):
        stageA1([(xblk[:], xkey, 0, D)], xb, ss, rs, tag)
        stageA2(xb, nw, ident, dstT, dcol0, dkey, tag)

    def gelu_ops(dst, t1, t2, k1, k2, kd, eng2='pool'):
        P.op(eng2, lambda e: e.tensor_tensor(t2, t1, t1, ALU.mult), reads=[k1], writes=[k2])
        P.op(eng2, lambda e: e.tensor_scalar(t2, t2, 0.044715, 1.0, ALU.mult, ALU.add), reads=[k2], writes=[k2])
        P.op(eng2, lambda e: e.tensor_tensor(t2, t2, t1, ALU.mult), reads=[k1, k2], writes=[k2])
        P.op('act', lambda e: e.activation(t2, t2, AF.Sigmoid, scale=GELU_K), reads=[k2], writes=[k2])
        P.op('dve', lambda e: e.tensor_tensor(dst, t1, t2, ALU.mult), reads=[k1, k2], writes=[kd])

    from contextlib import ExitStack
    with ExitStack() as es:
        def sb(name, shape, dt):
            return es.enter_context(nc.sbuf_tensor("s_" + name, list(shape), dt))

        KsT = sb("KsT", [96, 4, TV], BF16)
        KwT = sb("KwT", [96, 4, 1024], BF16)
        Vs = sb("Vs", [128, 32, 4, 65], BF16)
        Vw = sb("Vw", [128, 8, 4, 65], BF16)
        KcT = sb("KcT", [96, 4, 256], BF16)
        Vc = sb("Vc", [128, 2, 4, 65], BF16)
        h1v = sb("h1v", [128, 2, 4, 256], BF16)
        ident = sb("ident", [128, 128], BF16)
        maskc = sb("maskc", [128, 19, 128], BF16)
        cmask = sb("cmask", [128, 2, 128], BF16)
        EB = sb("EB", [96, TV], BF16)
        OVL = sb("OVL", [128, 2, 65], BF16)
        fb = sb("fb", [128, 17, 64], BF16)
        lnwT = sb("lnwT", [128, 8], F32)
        lnbT = sb("lnbT", [128, 8], F32)
        ones128 = sb("ones128", [128, 128], BF16)
        sbb = sb("sbb", [128, 8, 128], F32)
        swT = sb("swT", [128, 8, 128], BF16)
        nw1 = sb("nw1", [128, KC], F32)
        qw8 = sb("qw8", [64, 1], F32)
        kw3 = sb("kw3", [64, 3], F32)
        ones64 = sb("ones64", [96, 64], BF16)
        w1c = sb("w1c", [128, 2, 16, 256], BF16)
        w2c = sb("w2c", [128, 2, 2, 64], BF16)
        cposT = sb("cposT", [128, 2, 16], BF16)
        cbias = sb("cbias", [128, 4], F32)
        vval = sb("vval", [128, 32], BF16)
        vcval = sb("vcval", [128, 2], F32)
        gates = sb("gates", [128, 2, 48], F32)
        xblk = [sb("xblk%d" % i, [128, D], F32) for i in range(2)]
        xb = sb("xb", [128, D], BF16)
        junk = None
        ss = sb("ss", [128, 2], F32)
        rs = sb("rs", [128, 1], F32)
        xnTs = [sb("xnT%d" % i, [128, KC, TT], BF16) for i in range(2)]
        wbuf = [sb("wbuf%d" % i, [128, KC, 256], BF16) for i in range(2)]
        QT = sb("QT", [96, 16, TT], BF16)
        mixT = sb("mixT", [128, KC, TT], BF16)
        rawT2 = sb("rawT2", [128, 2, 4, 272], BF16)
        sq = sb("sq", [96, 512], BF16)
        rq = sb("rq", [64, 512], F32)
        vtm = [sb("vtm0", [128, 1024], F32)] * 2
        gt2 = sb("gt2", [128, 1024], F32)
        vn = sb("vn", [128, 1024], BF16)
        uT = sb("uT", [128, 8, TT], BF16)
        ut1 = sb("ut1", [128, TT], F32)
        ct1 = ut1[:].rearrange("p (a b) -> p a b", b=64)
        tmpo = ct1
        ct2 = gt2[:, 0:256].rearrange("p (a b) -> p a b", b=64)
        bnst = sb("bnst", [128, 2, 6], F32)
        bnag = sb("bnag", [128, 2], F32)
        lrs = sb("lrs", [128, 1], F32)
        PT = [sb("PT%d" % i, [128, 512], BF16) for i in range(4)]
        acc = [sb("acc%d" % i, [128, 4, 64], F32) for i in range(2)] * 2
        ablk = sb("ablk", [128, 1024], BF16)
        small = sb("small", [128, 64], F32)
        imp = sb("imp", [128, 64], F32)
        sc2 = sb("sc2", [128, 64], F32)
        m8 = sb("m8", [128, 16], F32)
        thr = sb("thr", [128, 1], F32)
        selbs = [sb("selb%d" % i, [128, 64], BF16) for i in range(2)]
        selT = [sb("selT%d" % i, [96, 128], BF16) for i in range(2)]
        h1 = sb("h1", [128, 4, 64], BF16)
        caus = None
        w1st = None

        def ld(dst, src, key, eng='sp'):
            P.dma(eng, dst, src, writes=[key])

        ld(ident[:], ident_d, 'ident')
        ld(nw1[:], nw1_d, 'nw')
        ld(qw8[:], qw_d, 'qw8')
        ld(kw3[:], kw_d, 'kw3')
        ld(ones64[:], ones_d, 'ones64')
        ld(maskc[:], maskc_d, 'maskc')
        ld(cmask[:], cmask_d, 'cmask')
        ld(EB[:], EB_d, 'EB')
        ld(OVL[:], OVL_d, 'OVL')
        ld(fb[:], fb_d, 'fb')
        ld(lnwT[:], lnw_d, 'lnwT')
        ld(lnbT[:], lnb_d, 'lnbT')
        ld(sbb[:], sbb_d, 'sbb')
        swf = gt2[:].rearrange("p (a b) -> p a b", b=128)
        ld(swf, swT_d, 'gt2')
        caus_ap = ut1[:, 0:128]
        ld(caus_ap, caus_d, 'ut1')
        ld(vval[:], vval_d, 'vval')
        ld(vcval[:], vcval_d, 'vcval')
        P.op('pool', lambda e: e.memset(KsT[64:96, :, :], 0.0), writes=['KsT'])
        P.op('pool', lambda e: e.memset(KwT[64:96, :, :], 0.0), writes=['KwT'])
        P.op('pool', lambda e: e.memset(KcT[64:96, :, :], 0.0), writes=['KcT'])
        P.op('pool', lambda e: e.memset(QT[64:96, :, :], 0.0), writes=['QT'])
        P.op('pool', lambda e: e.memset(sq[64:96, :], 0.0), writes=['sq'])
        for i_ in range(2):
            P.op('pool', lambda e, i_=i_: e.memset(selT[i_][64:96, :], 0.0), writes=['selT%d' % i_])
        for g in range(4):
            ld(KsT[64:70, g, :], kaugs_d, 'KsT')
            ld(KcT[64:70, g, :], kaugc_d, 'KcT')
        for kv in range(2):
            P.dma('pool', w1c[:, kv, :, :], cw1_d[kv].rearrange("(lp q) h -> q lp h", q=128), writes=['w1c'])
            P.dma('pool', w2c[:, kv, :, :], cw2_d[kv].rearrange("(hc q) d -> q hc d", q=128), writes=['w2c'])
        P.dma('pool', cposT[:], cpos_d, writes=['cposT'])
        P.op('dve', lambda e: e.tensor_scalar(qw8[:], qw8[:], 0.125, None, ALU.mult), reads=['qw8'], writes=['qw8'])
        cb_ap = tap(ut1, 0, 128, 0, [[0, 8], [1, 128]])
        P.op('dve', lambda e: e.tensor_tensor(swT[:], swf, cb_ap, ALU.mult), reads=['gt2', 'ut1'], writes=['swT'])
        P.op('pool', lambda e: e.memset(ones128[:], 1.0), writes=['ones128'])
        for hf_ in range(2):
            psw, pkw = nextA()
            P.op('pe', lambda e, psw=psw, hf_=hf_: e.matmul(psw[:], ones128[:], swT[:, hf_ * 4:(hf_ + 1) * 4, :], start=True, stop=True),
                 reads=['ones128', 'swT'], writes=[pkw])
            for gg_ in range(4):
                g_ = hf_ * 4 + gg_
                P.op('dve', lambda e, psw=psw, gg_=gg_, g_=g_: e.scalar_tensor_tensor(sbb[:, g_, :], psw[:, gg_ * 128:(gg_ + 1) * 128],
                                                                                  lnbT[:, g_:g_ + 1], sbb[:, g_, :], ALU.mult, ALU.add),
                     reads=[pkw, 'lnbT', 'sbb'], writes=['sbb'])
        P.op('pool', lambda e: e.memset(h1v[:], 0.0), writes=['h1v'])
        P.op('pool', lambda e: e.memset(KcT[0:64, :, :], 0.0), writes=['KcT'])
        P.op('pool', lambda e: e.memset(rawT2[:], 0.0), writes=['rawT2'])
        P.op('pool', lambda e: e.memset(Vc[:], 0.0), writes=['Vc'])
        for g in range(4):
            P.op('dve', lambda e, g=g: e.tensor_copy(Vs[:, :, g, 64], vval[:]), reads=['vval'], writes=['Vs'])
            P.op('dve', lambda e, g=g: e.tensor_copy(Vc[:, :, g, 64], vcval[:]), reads=['vcval', 'Vc'], writes=['Vc'])
        psb0, pkb0 = nextB()
        for kv in range(2):
            for hc in range(2):
                mm_group(psb0[:, (kv * 2 + hc):(kv * 2 + hc) + 1],
                         [(w1c[:, kv, lp, hc * 128:(hc + 1) * 128], cposT[:, kv, lp:lp + 1]) for lp in range(16)],
                         ['w1c', 'cposT'], [pkb0])
        P.op('act', lambda e: e.activation(cbias[:], psb0[:, 0:4], AF.Copy), reads=[pkb0], writes=['cbias'])

        wcnt = [0]
        cast_on = [0]

        def load_w(src_ap, ncols, keys):
            keys = [keys] if isinstance(keys, tuple) else keys
            i = wcnt[0] % 2
            wcnt[0] += 1
            key = 'wbuf%d' % i
            P.dma('sp', wbuf[i][:, :, 0:ncols], src_ap, reads=keys, writes=[key])
            if cast_on[0] == 2 or (cast_on[0] == 1 and wcnt[0] % 2 == 0):
                issue_casts(1, pace=[key])
            return wbuf[i], key

        def norm_fm(ps, pk, nsl, wcol, dst, dkey, extra_scale_key):
            n = nsl * 256
            P.op('act', lambda e: e.activation(sq[0:64, 0:n], ps[0:64, 0:n], AF.Square), reads=[pk], writes=['sq'])
            ps2, pk2 = nextB()
            P.op('pe', lambda e: e.matmul(ps2[0:64, 0:n], ones64[0:96, :], sq[0:96, 0:n], start=True, stop=True),
                 reads=['sq', 'ones64'], writes=[pk2])
            P.op('act', lambda e: e.activation(rq[:, 0:n], ps2[0:64, 0:n], AF.Ln, scale=1.0 / 64, bias=1e-6),
                 reads=[pk2], writes=['rq'])
            P.op('act', lambda e: e.activation(rq[:, 0:n], rq[:, 0:n], AF.Exp, scale=-0.5), reads=['rq'], writes=['rq'])
            src = ps[0:64, 0:n].rearrange("p (a b) -> p a b", b=256)
            rqv = rq[:, 0:n].rearrange("p (a b) -> p a b", b=256)
            P.op('dve', lambda e: e.scalar_tensor_tensor(dst, src, wcol, rqv, ALU.mult, ALU.mult),
                 reads=[pk, 'rq', extra_scale_key], writes=[dkey])

        def attention_block(qb, b):
            qi = qb - 15
            c0 = b * 128
            items = []
            order = ['c0', 'w0', 'c1', 's0', 'w1', 'c2', 's1', 'w2', 'c3', 's2', 'w3', 's3']
            nts = [0] if qb == 15 else [0, 1]
            for o in order:
                g = int(o[1])
                if o[0] == 'c':
                    for nt in nts:
                        items.append(('c', g, nt, nt == nts[0], nt == nts[-1]))
                elif o[0] == 'w':
                    kbs = list(range(qb - 4, qb + 1))
                    for kb in kbs:
                        items.append(('w', g, kb, kb == kbs[0], kb == kbs[-1]))
                else:
                    for kb in range(qb + 1):
                        items.append(('s', g, kb, kb == 0, kb == qb))

            def rhsQ(g):
                return tap(QT, 0, 96, (4 * g) * TT + c0, [[TT, 4], [1, 128]])

            st = {}

            def stage1(idx):
                kind, g, k, first, last = items[idx]
                if kind == 's' and first:
                    selection_finish(g)
                scnt[0] += 1
                ps, pk = [(psB[0], 'B0'), (psB[1], 'B1'), (psA[0], 'A0'), (psA[1], 'A1')][scnt[0] % 4]
                rq_ = rhsQ(g)
                if kind == 'c':
                    pairs = [(KcT[0:96, g, k * 128:(k + 1) * 128], rq_, ['KcT', 'QT'])]
                    mi = None
                    if k == 1:
                        mi = 2 + (qb - 16)
                    elif qb <= 16:
                        mi = qb - 15
                    else:
                        mi = 18
                    if mi is not None:
                        pairs.append((ident[:], tap(maskc, 0, 128, mi * 128, [[0, 4], [1, 128]]), ['ident', 'maskc']))
                elif kind == 'w':
                    slot = ((k // 2) % 4) * 256 + (k % 2) * 128
                    pairs = [(KwT[0:96, g, slot:slot + 128], rq_, ['KwT', 'QT'])]
                    d = qb - k
                    if d == 0:
                        pairs.append((ident[:], tap(cmask, 0, 128, 0, [[0, 4], [1, 128]]), ['ident', 'cmask']))
                    if d == 4:
                        pairs.append((ident[:], tap(cmask, 0, 128, 128, [[0, 4], [1, 128]]), ['ident', 'cmask']))
                else:
                    pairs = [(KsT[0:96, g, k * 128:(k + 1) * 128], rq_, ['KsT', 'QT']),
                             (EB[0:96, k * 128:(k + 1) * 128], tap(selT[g % 2], 0, 96, 0, [[0, 4], [1, 128]]),
                              ['EB', 'selT%d' % (g % 2)])]
                    if k == qb:
                        pairs.append((ident[:], tap(cmask, 0, 128, 0, [[0, 4], [1, 128]]), ['ident', 'cmask']))
                n = len(pairs)
                for i, (l, r, rd) in enumerate(pairs):
                    P.op('pe', lambda e, l=l, r=r, i=i, n=n, ps=ps: e.matmul(ps[:], l, r, start=(i == 0), stop=(i == n - 1)),
                         reads=rd, writes=[pk])
                pt = PT[idx % 4]
                ptk = 'PT%d' % (idx % 4)
                P.op('act', lambda e, pt=pt, ps=ps: e.activation(pt[:], ps[:], AF.Exp), reads=[pk], writes=[ptk])
                st[idx] = (pt, ptk)

            def stage3(idx):
                kind, g, k, first, last = items[idx]
                pt, ptk = st.pop(idx)
                if kind == 'c':
                    bI, kI, bC, kC = (psI, 'I', psC, 'C')
                    for r in range(4):
                        P.op('pe', lambda e, r=r, pt=pt, k=k, bI=bI: e.matmul(bI[:, r * 65:(r + 1) * 65], pt[:, r * 128:(r + 1) * 128],
                                                                              OVL[:, k, :], start=(first and r == 0), stop=(last and r == 3)),
                             reads=[ptk, 'OVL'], writes=[kI])
                    for r in range(4):
                        P.op('pe', lambda e, r=r, pt=pt, k=k, g=g, bC=bC: e.matmul(bC[:, r * 65:(r + 1) * 65], pt[:, r * 128:(r + 1) * 128],
                                                                                   Vc[:, k, g, :], start=(first and r == 0), stop=(last and r == 3)),
                             reads=[ptk, 'Vc'], writes=[kC])
                    if last:
                        selection(g, bI, kI)
                        fold(0, g, bC, kC)
                elif kind == 'w':
                    slot = k % 8
                    for r in range(4):
                        P.op('pe', lambda e, r=r, pt=pt, slot=slot, g=g: e.matmul(psW[:, r * 65:(r + 1) * 65], pt[:, r * 128:(r + 1) * 128],
                                                                                  Vw[:, slot, g, :], start=(first and r == 0), stop=(last and r == 3)),
                             reads=[ptk, 'Vw'], writes=['W'])
                    if last:
                        fold(2, g, psW, 'W')
                else:
                    for r in range(4):
                        P.op('pe', lambda e, r=r, pt=pt, k=k, g=g: e.matmul(psS[:, r * 65:(r + 1) * 65], pt[:, r * 128:(r + 1) * 128],
                                                                            Vs[:, k, g, :], start=(first and r == 0), stop=(last and r == 3)),
                             reads=[ptk, 'Vs'], writes=['S'])
                    if last:
                        fold(1, g, psS, 'S')

            def selection(g, psI, kI):
                den = tap(psI, 0, 128, 64, [[65, 4]])
                rdn = small[:, 0:4]
                P.op('dve', lambda e: e.tensor_scalar(rdn, den, 1e-30, None, ALU.max), reads=[kI], writes=['small'])
                P.op('dve', lambda e: e.reciprocal(rdn, rdn), reads=['small'], writes=['small'])
                P.op('dve', lambda e: e.tensor_scalar(imp[:], psI[:, 0:64], small[:, 0:1], None, ALU.mult),
                     reads=[kI, 'small'], writes=['imp'])
                for r in range(1, 4):
                    P.op('dve', lambda e, r=r: e.scalar_tensor_tensor(imp[:], psI[:, r * 65:r * 65 + 64], small[:, r:r + 1], imp[:],
                                                                      ALU.mult, ALU.add),
                         reads=[kI, 'small', 'imp'], writes=['imp'])
                P.op('dve', lambda e: e.tensor_tensor(imp[:], imp[:], fb[:, qi, :], ALU.add), reads=['imp', 'fb'], writes=['imp'])
                P.op('dve', lambda e: e.max(m8[:, 0:8], imp[:]), reads=['imp'], writes=['m8'])
                P.op('dve', lambda e: e.match_replace(sc2[:], m8[:, 0:8], imp[:], -3e9), reads=['imp', 'm8'], writes=['sc2'])
                P.op('dve', lambda e: e.max(m8[:, 8:16], sc2[:]), reads=['sc2'], writes=['m8'])
                P.op('dve', lambda e: e.tensor_scalar(thr[:], m8[:, 15:16], -0.5e9, None, ALU.max), reads=['m8'], writes=['thr'])
                sb_ = selbs[g % 2]
                P.op('dve', lambda e: e.tensor_scalar(sb_[:], imp[:], thr[:, 0:1], NEG, ALU.is_lt, ALU.mult),
                     reads=['imp', 'thr'], writes=['selb%d' % (g % 2)])

            def selection_finish(g):
                ps, pk = nextB()
                psb = ps[:].bitcast(BF16)
                sb_ = selbs[g % 2]
                P.op('pe', lambda e: e.transpose(psb[0:64, 0:128], sb_[:], ident[:]), reads=['selb%d' % (g % 2), 'ident'], writes=[pk])
                sT = selT[g % 2]
                P.op('act', lambda e: e.activation(sT[0:64, :], psb[0:64, 0:128], AF.Copy), reads=[pk], writes=['selT%d' % (g % 2)])

            def fold(br, g, ps, pk):
                o = 8 + br * 8
                rd = small[:, o:o + 4]
                gw = small[:, o + 4:o + 8]
                den = tap(ps, 0, 128, 64, [[65, 4]])
                P.op('dve', lambda e: e.tensor_scalar(rd, den, 1e-30, None, ALU.max), reads=[pk], writes=['small'])
                P.op('dve', lambda e: e.reciprocal(rd, rd), reads=['small'], writes=['small'])
                gsel = tap(gates, 0, 128, b * 48 + g * 12 + br, [[3, 4]])
                P.op('dve', lambda e: e.tensor_tensor(gw, rd, gsel, ALU.mult), reads=['small', 'gates'], writes=['small'])
                src = tap(ps, 0, 128, 0, [[65, 4], [1, 64]])
                gwb = tap(small, 0, 128, o + 4, [[1, 4], [0, 64]])
                akey = 'acc%d' % (g % 2)
                if br == 0:
                    P.op('dve', lambda e: e.tensor_tensor(acc[g][:], src, gwb, ALU.mult), reads=[pk, 'small'], writes=[akey])
                elif br == 2:
                    P.op('dve', lambda e: e.tensor_tensor(tmpo, src, gwb, ALU.mult), reads=[pk, 'small'], writes=['ut1'])
                    P.op('pool', lambda e: e.tensor_tensor(acc[g][:], acc[g][:], tmpo, ALU.add), reads=['ut1', akey], writes=[akey])
                else:
                    P.op('dve', lambda e: e.tensor_tensor(tmpo, src, gwb, ALU.mult), reads=[pk, 'small'], writes=['ut1'])
                    dst = ablk[:, g * 256:(g + 1) * 256].rearrange("p (a b) -> p a b", b=64)
                    P.op('pool', lambda e: e.tensor_tensor(dst, acc[g][:], tmpo, ALU.add), reads=['ut1', akey], writes=['ablk'])

            n = len(items)
            LAG = 2
            for idx in range(n + LAG):
                if idx < n:
                    stage1(idx)
                if idx >= LAG:
                    stage3(idx - LAG)
            if dbg_d is not None:
                P.dma('pool', dbg_d[:, qi, 0:1024], ablk[:], reads=['ablk'], writes=[('dbg', qi, 0)], final=True)
                P.dma('pool', dbg_d[:, qi, 1024:2048].rearrange("p (a b) -> p a b", b=128), mixT[:, 8:16, c0:c0 + 128],
                      reads=['mixT'], writes=[('dbg', qi, 1)], final=True)
            ps, pk = nextB()
            psb = ps[:].bitcast(BF16)
            for k in range(8):
                P.op('pe', lambda e, k=k: e.transpose(psb[:, k * 128:(k + 1) * 128], ablk[:, k * 128:(k + 1) * 128], ident[:]),
                     reads=['ablk', 'ident'], writes=[pk])
            P.op('act', lambda e: e.activation(mixT[:, 0:8, c0:c0 + 128], psb.rearrange("p (a b) -> p a b", b=128), AF.Copy),
                 reads=[pk], writes=['mixT'])

        def prefetch_A1(tn, b):
            blk = 2 * tn + b
            P.dma('sp', vtm[0][:], xv[blk * 128:(blk + 1) * 128, 0:1024], writes=['vtm'])
            P.dma('sp', gt2[:], xv[blk * 128:(blk + 1) * 128, 1024:2048], writes=['gt2'])
            stageA1([(vtm[0][:], 'vtm', 0, 1024), (gt2[:], 'gt2', 1024, 1024)], xb, ss, rs, '')

        def prefetch_A2(tn, b):
            stageA2(xb, nw1, ident, xnTs[tn % 2], b * 128, 'xnT%d' % (tn % 2), '')

        for b_ in range(2):
            prefetch_A1(0, b_)
            prefetch_A2(0, b_)
        for t in range(NTILE):
            own = t >= 8
            has_q = own or t == 7
            qblocks = [15] if t == 7 else ([2 * t, 2 * t + 1] if own else [])
            cast_on[0] = 1 if t >= 7 else 2
            xnT = xnTs[t % 2]
            XK = 'xnT%d' % (t % 2)
            if has_q:
                for b in ([1] if t == 7 else [0, 1]):
                    blk = 2 * t + b
                    P.dma('pool', xblk[b][:], xv[blk * 128:(blk + 1) * 128, :], writes=['xblk%d' % b])
            if has_q:
                P.dma('sp', QT[64:70, :, :], qaug_d[:, :, t * TT:(t + 1) * TT], writes=['QT'])
            slotw = (t % 4) * 256
            for g in range(4):
                P.dma('sp', KwT[64:70, g, slotw:slotw + 256], kaugs_d[:, t * TT:(t + 1) * TT], writes=['KwT'])
            rhs_all = [xnT[:, kc, :] for kc in range(KC)]
            if has_q:
                for j in range(4):
                    wb, wk = load_w(win_src(C_Q + j * 256, 256), 256, ('win', C_Q + j * 256))
                    for hp in range(2):
                        ps, pk = nextA()
                        for hh in range(2):
                            hl = 2 * hp + hh
                            mm_group(ps[0:64, hh * 256:(hh + 1) * 256],
                                     [(wb[:, kc, hl * 64:(hl + 1) * 64], rhs_all[kc]) for kc in range(KC)],
                                     [wk, XK], [pk])
                        h0 = 4 * j + 2 * hp
                        norm_fm(ps, pk, 2, qw8[:, 0:1], QT[0:64, h0:h0 + 2, :], 'QT', 'qw8')
            P.op('dve', lambda e: e.tensor_copy(rawT2[:, :, :, 0:16], rawT2[:, :, :, 256:272]), reads=['rawT2'], writes=['rawT2'])
            for kv in range(2):
                wb, wk = load_w(win_src(C_KC + kv * 256, 256), 256, ('win', C_KC + kv * 256))
                for gp in range(2):
                    ps, pk = nextA()
                    for gg in range(2):
                        g = 2 * gp + gg
                        mm_group(ps[0:64, gg * 256:(gg + 1) * 256],
                                 [(wb[:, kc, g * 64:(g + 1) * 64], rhs_all[kc]) for kc in range(KC)],
                                 [wk, XK], [pk])
                    P.op('act', lambda e, ps=ps, kv=kv, gp=gp: e.activation(
                        rawT2[0:64, kv, 2 * gp:2 * gp + 2, 16:272], ps[0:64, :].rearrange("p (a b) -> p a b", b=256), AF.Copy),
                        reads=[pk], writes=['rawT2'])
            P.dma('sp', rawT2[64:128, :, :, 15:271], rawT2[0:64, :, :, 16:272], reads=['rawT2'], writes=['rawT2'])
            for br, c_off in ((1, C_KS), (2, C_KW)):
                if br == 2 and t < 5:
                    continue
                wb, wk = load_w(win_src(c_off, 256), 256, ('win', c_off))
                for gp in range(2):
                    ps, pk = nextA()
                    for gg in range(2):
                        g = 2 * gp + gg
                        mm_group(ps[0:64, gg * 256:(gg + 1) * 256],
                                 [(wb[:, kc, g * 64:(g + 1) * 64], rhs_all[kc]) for kc in range(KC)],
                                 [wk, XK], [pk])
                    if br == 1:
                        dst = KsT[0:64, 2 * gp:2 * gp + 2, t * TT:(t + 1) * TT]
                        norm_fm(ps, pk, 2, kw3[:, 1:2], dst, 'KsT', 'kw3')
                    else:
                        dst = KwT[0:64, 2 * gp:2 * gp + 2, slotw:slotw + 256]
                        norm_fm(ps, pk, 2, kw3[:, 2:3], dst, 'KwT', 'kw3')
            for br, c_off in ((1, C_VS), (2, C_VW)):
                if br == 2 and t < 5:
                    continue
                wb, wk = load_w(win_src(c_off, 256), 256, ('win', c_off))
                for b in range(2):
                    blk = 2 * t + b
                    ps, pk = nextA()
                    mm_group(ps[:, 0:256], [(xnT[:, kc, b * 128:(b + 1) * 128], wb[:, kc, :]) for kc in range(KC)],
                             [wk, XK], [pk])
                    src = ps[:, 0:256].rearrange("p (a b) -> p a b", b=64)
                    if br == 1:
                        P.op('act', lambda e, src=src, blk=blk: e.activation(Vs[:, blk, :, 0:64], src, AF.Copy),
                             reads=[pk], writes=['Vs'])
                    else:
                        slot = blk % 8
                        P.op('act', lambda e, src=src, slot=slot: e.activation(Vw[:, slot, :, 0:64], src, AF.Copy),
                             reads=[pk], writes=['Vw'])
                        vb = tap(vval, 0, 128, blk, [[0, 4]])
                        P.op('dve', lambda e, slot=slot, vb=vb: e.tensor_copy(Vw[:, slot, :, 64], vb), reads=['vval'], writes=['Vw'])
            if not has_q and t + 1 < NTILE:
                for b_ in range(2):
                    prefetch_A1(t + 1, b_)
                    prefetch_A2(t + 1, b_)
            n0 = 16 * t - 1
            psc, pkc = nextA()
            for kv in range(2):
                for hc in range(2):
                    a = kv * 2 + hc
                    mm_group(psc[:, a * 64:(a + 1) * 64],
                             [(w1c[:, kv, lp, hc * 128:(hc + 1) * 128], tap(rawT2, 0, 128, kv * 4 * 272 + 2 * lp, [[272, 4], [16, 16]]))
                              for lp in range(16)], ['w1c', 'rawT2'], [pkc])
            for a in range(4):
                P.op('act', lambda e, a=a, psc=psc: e.activation(ct1[:, a, :], psc[:, a * 64:(a + 1) * 64], AF.Identity, bias=cbias[:, a:a + 1]),
                     reads=[pkc, 'cbias'], writes=['ut1'])
            gelu_ops(h1[:], ct1, ct2, 'ut1', 'gt2', 'h1')
            m0 = 1 if t == 0 else 0
            nm = 16 - m0
            psk, pkk = nextA()
            mm_group(psk[0:64, 0:64], [(w2c[:, 0, hc, :], h1[:, hc, :]) for hc in range(2)], ['w2c', 'h1'], [pkk])
            P.op('act', lambda e, psk=psk: e.activation(sq[0:64, 0:64], psk[0:64, 0:64], AF.Square), reads=[pkk], writes=['sq'])
            ps2, pk2 = nextB()
            P.op('pe', lambda e, ps2=ps2: e.matmul(ps2[0:64, 0:64], ones64[0:96, :], sq[0:96, 0:64], start=True, stop=True),
                 reads=['sq', 'ones64'], writes=[pk2])
            P.op('act', lambda e, ps2=ps2: e.activation(rq[:, 0:64], ps2[0:64, 0:64], AF.Ln, scale=1.0 / 64, bias=1e-6),
                 reads=[pk2], writes=['rq'])
            P.op('act', lambda e: e.activation(rq[:, 0:64], rq[:, 0:64], AF.Exp, scale=-0.5), reads=['rq'], writes=['rq'])
            srck = tap(psk, 0, 64, m0, [[16, 4], [1, nm]])
            rqk = tap(rq, 0, 64, m0, [[16, 4], [1, nm]])
            dstk = tap(KcT, 0, 64, n0 + m0, [[256, 4], [1, nm]])
            P.op('dve', lambda e, srck=srck, rqk=rqk, dstk=dstk: e.scalar_tensor_tensor(dstk, srck, kw3[:, 0:1], rqk, ALU.mult, ALU.mult),
                 reads=[pkk, 'rq', 'kw3'], writes=['KcT'])
            for hc in range(2):
                srcv = tap(h1, 0, 128, (2 + hc) * 64 + m0, [[16, 4], [1, nm]])
                dstv = tap(h1v, 0, 128, hc * 1024 + n0 + m0, [[256, 4], [1, nm]])
                P.op('pool', lambda e, srcv=srcv, dstv=dstv: e.tensor_copy(dstv, srcv), reads=['h1'], writes=['h1v'])
            for nt in sorted(set([max(n0, 0) // 128, (n0 + 15) // 128])):
                psv, pkv = nextA()
                for g in range(4):
                    mm_group(psv[:, g * 64:(g + 1) * 64],
                             [(h1v[:, hc, g, nt * 128:(nt + 1) * 128], w2c[:, 1, hc, :]) for hc in range(2)],
                             ['h1v', 'w2c'], [pkv])
                P.op('dve', lambda e, psv=psv, nt=nt: e.tensor_scalar(Vc[:, nt, :, 0:64], psv[:, 0:256].rearrange("p (a b) -> p a b", b=64),
                                                                     vcval[:, nt:nt + 1], None, ALU.mult),
                     reads=[pkv, 'vcval'], writes=['Vc'])
            if not has_q:
                continue
            blks = [1] if t == 7 else [0, 1]
            wb, wk = load_w(win_src(C_GL, 48), 48, ('win', C_GL))
            for b in blks:
                ps, pk = nextA()
                mm_group(ps[:, 0:48], [(xnT[:, kc, b * 128:(b + 1) * 128], wb[:, kc, 0:48]) for kc in range(KC)],
                         [wk, XK], [pk])
                P.op('act', lambda e, ps=ps, b=b: e.activation(gates[:, b, :], ps[:, 0:48], AF.Sigmoid), reads=[pk], writes=['gates'])
            for j in range(4):
                wb, wk = load_w(win_src(C_U + j * 256, 256), 256, ('win', C_U + j * 256))
                for cc in range(2):
                    ps, pk = nextA()
                    mm_group(ps[:, 0:256], [(wb[:, kc, cc * 128:(cc + 1) * 128], rhs_all[kc]) for kc in range(KC)],
                             [wk, XK], [pk])
                    P.op('act', lambda e, ps=ps: e.activation(ut1[:], ps[:, 0:256], AF.Copy), reads=[pk], writes=['ut1'])
                    gelu_ops(uT[:, 2 * j + cc, :], ut1[:], gt2[:, 0:256], 'ut1', 'gt2', 'uT')
            for b in blks:
                vk = 'vtm'
                for j in range(4):
                    wb, wk = load_w(win_src(C_V + j * 256, 256), 256, ('win', C_V + j * 256))
                    ps, pk = nextA()
                    mm_group(ps[:, 0:256], [(xnT[:, kc, b * 128:(b + 1) * 128], wb[:, kc, :]) for kc in range(KC)],
                             [wk, XK], [pk])
                    P.op('act', lambda e, ps=ps, b=b, j=j: e.activation(vtm[b][:, j * 256:(j + 1) * 256], ps[:, 0:256], AF.Copy),
                         reads=[pk], writes=[vk, ('vtm', j), 'gt2'])
                    gelu_ops(vtm[b][:, j * 256:(j + 1) * 256], vtm[b][:, j * 256:(j + 1) * 256], gt2[:, j * 256:(j + 1) * 256],
                             ('vtm', j), ('gt2', j), ('vtm', j))
                VP = [('vtm', j_) for j_ in range(4)]
                GP = [('gt2', j_) for j_ in range(4)]
                for c in range(2):
                    P.op('dve', lambda e, b=b, c=c: e.bn_stats(bnst[:, c, :], vtm[b][:, c * 512:(c + 1) * 512]), reads=[vk] + VP, writes=['bnst'])
                P.op('dve', lambda e: e.bn_aggr(bnag[:], bnst[:].rearrange("p a b -> p (a b)")), reads=['bnst'], writes=['bnag'])
                P.op('act', lambda e: e.activation(lrs[:], bnag[:, 1:2], AF.Sqrt, bias=1e-5), reads=['bnag'], writes=['lrs'])
                P.op('dve', lambda e: e.reciprocal(lrs[:], lrs[:]), reads=['lrs'], writes=['lrs'])
                P.op('dve', lambda e, b=b: e.tensor_scalar(vn[:], vtm[b][:], bnag[:, 0:1], lrs[:, 0:1], ALU.subtract, ALU.mult),
                     reads=[vk, 'bnag', 'lrs'] + VP + GP, writes=['vn', 'gt2', vk])
                attention_block(2 * t + b, b)
                for hf in range(2):
                    ps, pk = nextA()
                    for gg in range(4):
                        g = hf * 4 + gg
                        P.op('pe', lambda e, ps=ps, g=g, gg=gg: e.matmul(ps[:, gg * 128:(gg + 1) * 128], vn[:, g * 128:(g + 1) * 128],
                                                                         swT[:, g, :], start=True, stop=True),
                             reads=['vn', 'swT'], writes=[pk])
                    t3 = gt2[:, 0:512].rearrange("p (a b) -> p a b", b=128)
                    for gg in range(4):
                        g = hf * 4 + gg
                        P.op('dve', lambda e, ps=ps, gg=gg, g=g: e.scalar_tensor_tensor(gt2[:, gg * 128:(gg + 1) * 128], ps[:, gg * 128:(gg + 1) * 128],
                                                                                      lnwT[:, g:g + 1], sbb[:, g, :], ALU.mult, ALU.add),
                             reads=[pk, 'sbb', 'lnwT'], writes=['gt2'])
                    P.op('dve', lambda e, hf=hf, b=b, t3=t3: e.tensor_tensor(mixT[:, 8 + hf * 4:8 + (hf + 1) * 4, b * 128:(b + 1) * 128], t3,
                                                                             uT[:, hf * 4:(hf + 1) * 4, b * 128:(b + 1) * 128], ALU.mult),
                         reads=['gt2', 'uT'], writes=['mixT'])
            for j in range(8):
                if t + 1 < NTILE:
                    if j == 0:
                        prefetch_A1(t + 1, 0)
                    if j == 4:
                        prefetch_A2(t + 1, 0)
                        prefetch_A1(t + 1, 1)
                wb, wk = load_w(wout_src(j), 256, ('wout', j))
                for b in blks:
                    ps, pk = nextA()
                    mm_group(ps[:, 0:256], [(mixT[:, kc, b * 128:(b + 1) * 128], wb[:, kc, :]) for kc in range(KC)],
                             [wk, 'mixT'], [pk])
                    P.op('dve', lambda e, ps=ps, b=b, j=j: e.tensor_tensor(xblk[b][:, j * 256:(j + 1) * 256], xblk[b][:, j * 256:(j + 1) * 256],
                                                                           ps[:, 0:256], ALU.add),
                         reads=[pk, 'xblk%d' % b], writes=['xblk%d' % b])
            if t + 1 < NTILE:
                prefetch_A2(t + 1, 1)
            for b in blks:
                blk = 2 * t + b
                if blk == 15:
                    P.dma('pool', x1h_d, xblk[b][:], reads=['xblk%d' % b], writes=['x1h'])
                else:
                    r0 = (blk - 16) * 128
                    P.dma('pool', out_d[r0:r0 + 128, :], xblk[b][:], reads=['xblk%d' % b], writes=[('out', blk - 16)],
                          final=stop_after_mixer)
        issue_casts(1000)
        P.barrier()
        P.emit()

    if stop_after_mixer:
        P.finish()
        return nc

    with ExitStack() as es:
        def sb(name, shape, dt):
            return es.enter_context(nc.sbuf_tensor("s_" + name, list(shape), dt))

        ident = sb("ident2", [128, 128], BF16)
        nw2 = sb("nw2", [128, KC], F32)
        cw = sb("cw", [128, 3, 88], F32)
        cb = sb("cb", [128, 88], F32)
        hprev = sb("hprev", [128, 88, 2], F32)
        x1b = [sb("x1b%d" % i, [128, D], F32) for i in range(4)]
        xb = sb("xb2", [128, D], BF16)
        junk = None
        ss = sb("ss2", [128, 2], F32)
        rs = sb("rs2", [128, 1], F32)
        xn2T = sb("xn2T", [128, KC, 512], BF16)
        xnhT = sb("xnhT", [128, KC, 128], BF16)
        yT = sb("yT", [128, NFC, 512], BF16)
        wu = [sb("wu%d" % i, [128, 2, KC, 256], BF16) for i in range(2)]
        wd = [sb("wd%d" % i, [128, 11, 512], BF16) for i in range(2)]
        hb = [sb("hb%d" % i, [128, 514], F32) for i in range(2)]
        ac = [sb("ac%d" % i, [128, 512], F32) for i in range(2)]
        sg = sb("sg", [128, 512], F32)

        P.dma('sp', ident[:], ident_d, writes=['ident'])
        P.dma('sp', nw2[:], nw2_d, writes=['nw'])
        P.dma('sp', cw[:], cw_d, writes=['cw'])
        P.dma('sp', cb[:], cb_d, writes=['cb'])
        P.dma('sp', x1b[0][:], x1h_d, reads=['x1h'], writes=['x1b0'])
        rmsnorm_T(x1b[0], 'x1b0', xb, junk, ss, rs, nw2, ident, xnhT, 0, 'xnhT', '2')
        psD = [psC, psI, psS, psW]
        psDk = ['C', 'I', 'S', 'W']
        wucnt = [0]
        wdcnt = [0]
        for ft in range(4):
            for b in range(4):
                r0 = ft * 512 + b * 128
                P.dma('sp', x1b[b][:], out_d[r0:r0 + 128, :], reads=[('out', ft * 4 + b)], writes=['x1b%d' % b])
                rmsnorm_T(x1b[b], 'x1b%d' % b, xb, junk, ss, rs, nw2, ident, xn2T, b * 128, 'xn2T', '2')
            for cg in range(22):
                i = wucnt[0] % 2
                wucnt[0] += 1
                wuk = 'wu%d' % i
                for hf in range(2):
                    c_off = hf * DFF + cg * 256
                    P.dma('sp', wu[i][:, hf, :, :], wup_g[hf, cg].rearrange("p (k c) -> p k c", c=256),
                          reads=[('wup', hf, cg)], writes=[wuk])
                for cc in range(2):
                    c = 2 * cg + cc
                    accs = []
                    for hf in range(2):
                        ch = hf * NFC + c
                        if ft == 0:
                            psh, pkh = nextB()
                            mm_group(psh[:, 0:2], [(wu[i][:, hf, kc, cc * 128:(cc + 1) * 128], xnhT[:, kc, 126:128]) for kc in range(KC)],
                                     [wuk, 'xnhT'], [pkh])
                            P.op('act', lambda e, psh=psh, ch=ch: e.activation(hprev[:, ch, :], psh[:, 0:2], AF.Copy),
                                 reads=[pkh], writes=['hprev'])
                        ps, pk = nextA()
                        mm_group(ps[:], [(wu[i][:, hf, kc, cc * 128:(cc + 1) * 128], xn2T[:, kc, :]) for kc in range(KC)],
                                 [wuk, 'xn2T'], [pk])
                        h_ = hb[hf]
                        hk = 'hb%d' % hf
                        a_ = ac[hf]
                        ak = 'ac%d' % hf
                        P.op('pool', lambda e, h_=h_, ch=ch: e.tensor_copy(h_[:, 0:2], hprev[:, ch, :]), reads=['hprev'], writes=[hk])
                        P.op('act', lambda e, h_=h_, ps=ps: e.activation(h_[:, 2:514], ps[:], AF.Copy), reads=[pk], writes=[hk])
                        P.op('pool', lambda e, h_=h_, ch=ch: e.tensor_copy(hprev[:, ch, :], h_[:, 512:514]), reads=[hk], writes=['hprev'])
                        P.op('act', lambda e, a_=a_, ps=ps, ch=ch: e.activation(a_[:], ps[:], AF.Identity, bias=cb[:, ch:ch + 1],
                                                                              scale=cw[:, 2, ch:ch + 1]),
                             reads=[pk, 'cw', 'cb'], writes=[ak])
                        P.op('dve', lambda e, a_=a_, h_=h_, ch=ch: e.scalar_tensor_tensor(a_[:], h_[:, 1:513], cw[:, 1, ch:ch + 1], a_[:],
                                                                                       ALU.mult, ALU.add),
                             reads=[hk, ak, 'cw'], writes=[ak])
                        P.op('dve', lambda e, a_=a_, h_=h_, ch=ch: e.scalar_tensor_tensor(a_[:], h_[:, 0:512], cw[:, 0, ch:ch + 1], a_[:],
                                                                                       ALU.mult, ALU.add),
                             reads=[hk, ak, 'cw'], writes=[ak])
                    P.op('act', lambda e: e.activation(sg[:], ac[0][:], AF.Silu), reads=['ac0'], writes=['sg'])
                    P.op('dve', lambda e, c=c: e.tensor_tensor(yT[:, c, :], sg[:], ac[1][:], ALU.mult), reads=['sg', 'ac1'], writes=['yT'])
            for ng in range(4):
                for s in range(4):
                    i = wdcnt[0] % 2
                    wdcnt[0] += 1
                    wdk = 'wd%d' % i
                    P.dma('sp', wd[i][:], wdn_g[ng, s].rearrange("p (k c) -> p k c", c=512),
                          reads=[('wdn', ng, s)], writes=[wdk])
                    for b in range(4):
                        for k in range(11):
                            kk = s * 11 + k
                            P.op('pe', lambda e, b=b, k=k, kk=kk, i=i: e.matmul(psD[b][:], yT[:, kk, b * 128:(b + 1) * 128], wd[i][:, k, :],
                                                                                start=(kk == 0), stop=(kk == NFC - 1)),
                                 reads=['yT', wdk], writes=[psDk[b]])
                for b in range(4):
                    P.op('dve', lambda e, b=b, ng=ng: e.tensor_tensor(x1b[b][:, ng * 512:(ng + 1) * 512], x1b[b][:, ng * 512:(ng + 1) * 512],
                                                                     psD[b][:], ALU.add),
                         reads=[psDk[b], 'x1b%d' % b], writes=['x1b%d' % b])
            for b in range(4):
                r0 = ft * 512 + b * 128
                P.dma('pool', out_d[r0:r0 + 128, :], x1b[b][:], reads=['x1b%d' % b], writes=[('out', ft * 4 + b)], final=True)
        P.barrier()
        P.emit()
    P.finish()
    return nc


_bf = ml_dtypes.bfloat16


def _bf_split(v):
    hi = v.astype(np.float32).astype(_bf)
    lo = (v.astype(np.float64) - hi.astype(np.float64)).astype(np.float32).astype(_bf)
    return hi, lo


def _const_tables(half):
    c = {}
    slopes = np.power(2.0, -8.0 * np.arange(1, 17) / 16).astype(np.float32)
    s_hi, s_lo = _bf_split(slopes)
    tpos = np.arange(TV)
    cc = -(slopes.astype(np.float64)[:, None] * tpos[None, :])
    c_hi, c_lo = _bf_split(cc)
    qaug = np.zeros((6, 16, TV), dtype=_bf)
    qaug[0] = s_hi[:, None]
    qaug[1] = s_hi[:, None]
    qaug[2] = s_lo[:, None]
    qaug[3] = s_lo[:, None]
    qaug[4] = c_hi
    qaug[5] = c_lo
    c["qaug"] = qaug
    ka = np.zeros((6, TV), dtype=np.float32)
    ka[0] = (tpos // 64) * 64
    ka[1] = tpos % 64
    ka[2] = ka[0]
    ka[3] = ka[1]
    ka[4] = 1.0
    ka[5] = 1.0
    c["kaugs"] = ka.astype(_bf)
    n = np.arange(256)
    kc = np.zeros((6, 256), dtype=np.float32)
    kc[0] = 16 * n
    kc[1] = 15.5
    kc[2] = 16 * n
    kc[3] = 15.5
    kc[4] = 1.0
    kc[5] = 1.0
    c["kaugc"] = kc.astype(_bf)
    pats = [(15, 0), (16, 0)] + [(q_, 1) for q_ in range(16, 32)]
    mk = np.zeros((128, 19, 128), dtype=np.float32)
    for pi, (q_, nt_) in enumerate(pats):
        nn = (128 * nt_ + np.arange(128))[:, None]
        tt = (128 * q_ + np.arange(128))[None, :]
        okc = (16 * nn + 31 <= tt) & (nn <= 254) & ((half == 1) | (nn >= 128))
        mk[:, pi, :] = np.where(okc, 0.0, NEG)
    mk[:, 18, :] = 0.0 if half == 1 else NEG
    c["maskc"] = mk.astype(_bf)
    j = np.arange(128)[:, None]
    ii = np.arange(128)[None, :]
    cm = np.zeros((128, 2, 128), dtype=np.float32)
    cm[:, 0, :] = np.where(j <= ii, 0.0, NEG)
    cm[:, 1, :] = np.where(j > ii, 0.0, NEG)
    c["cmask"] = cm.astype(_bf)
    eb = np.zeros((96, TV), dtype=np.float32)
    eb[tpos // 64, tpos] = 1.0
    c["EB"] = eb.astype(_bf)
    ovl = np.zeros((128, 2, 65), dtype=np.float32)
    for t2 in range(2):
        nn2 = 128 * t2 + np.arange(128)
        start = (nn2 * 16)[:, None]
        s0 = (np.arange(64) * 64)[None, :]
        ov = ((start < s0 + 64) & (start + 32 > s0)).astype(np.float32)
        ok = (nn2 <= 254)
        ov = ov * ok[:, None]
        vc = ok & ((half == 1) | (nn2 >= 128))
        ovl[:, t2, 0:64] = ov
        ovl[:, t2, 64] = vc
    c["OVL"] = ovl.astype(_bf)
    c["vcval"] = np.ascontiguousarray(ovl[:, :, 64]).astype(np.float32)
    first = 0 if half == 1 else 32
    fbm = np.zeros((128, 17, 64), dtype=np.float32)
    for qi in range(17):
        t3 = 128 * (15 + qi) + np.arange(128)
        cur = (t3 // 64)[:, None]
        jj = np.arange(64)[None, :]
        valid = (jj <= cur) & (jj >= first)
        forced = (jj == first) | (jj == cur) | (jj == cur - 1)
        fbm[:, qi, :] = np.where(valid, np.where(forced, 1e9, 0.0), -1e9)
    c["fb"] = fbm.astype(_bf)
    vv = np.zeros((128, 32), dtype=np.float32)
    vv[:, :] = ((half == 1) | (np.arange(32) >= 16))[None, :]
    c["vval"] = vv.astype(_bf)
    c["caus"] = (j <= ii).astype(np.float32)
    o96 = np.zeros((96, 64), dtype=np.float32)
    o96[0:64] = 1.0
    c["ones64"] = o96.astype(_bf)
    c["ident"] = np.eye(128, dtype=np.float32).astype(_bf)
    return c


def _shared_inputs(inp):
    f = np.float32
    s = {}
    s["w_in"] = np.ascontiguousarray(inp["w_in"], dtype=f)
    s["w_out"] = np.ascontiguousarray(inp["w_out"], dtype=f)
    s["w_up"] = np.ascontiguousarray(inp["w_up"], dtype=f)
    s["w_down"] = np.ascontiguousarray(inp["w_down"], dtype=f)
    s["nw1"] = np.ascontiguousarray(np.asarray(inp["attn_norm_w"], dtype=f).reshape(KC, 128).T)
    s["nw2"] = np.ascontiguousarray(np.asarray(inp["ffn_norm_w"], dtype=f).reshape(KC, 128).T)
    s["qw"] = np.ascontiguousarray(np.asarray(inp["q_norm_w"], dtype=f).reshape(64, 1))
    s["kw"] = np.ascontiguousarray(np.asarray(inp["k_norm_w"], dtype=f).T)
    cp = np.asarray(inp["cmp_pos"], dtype=f)
    s["cpos"] = np.ascontiguousarray(cp.reshape(2, 16, 2, 64).transpose(2, 3, 0, 1).reshape(128, 2, 16))
    s["cw1"] = np.ascontiguousarray(inp["cmp_w1"], dtype=f)
    s["cw2"] = np.ascontiguousarray(inp["cmp_w2"], dtype=f)
    s["lnw"] = np.ascontiguousarray(np.asarray(inp["gmlp_ln_w"], dtype=f).reshape(8, 128).T)
    s["lnb"] = np.ascontiguousarray(np.asarray(inp["gmlp_ln_b"], dtype=f).reshape(8, 128).T)
    s["swT"] = np.ascontiguousarray(np.asarray(inp["spatial_w"], dtype=f).transpose(2, 0, 1))
    s["sbb"] = np.ascontiguousarray(np.broadcast_to(np.asarray(inp["spatial_b"], dtype=f)[None, :, :], (128, 8, 128)))
    s["cw"] = np.ascontiguousarray(np.asarray(inp["conv_w"], dtype=f).reshape(3, 88, 128).transpose(2, 0, 1))
    s["cb"] = np.ascontiguousarray(np.asarray(inp["conv_b"], dtype=f).reshape(88, 128).T)
    return s


def make_in_maps(inp):
    x = np.asarray(inp["x"], dtype=np.float32)
    shared = _shared_inputs(inp)
    consts = [_const_tables(0), _const_tables(1)]
    maps = []
    for core in range(8):
        b, half = core // 2, core % 2
        m = dict(shared)
        m.update(consts[half])
        if half == 0:
            xv = np.zeros((TV, D), dtype=np.float32)
            xv[OWN0:] = x[b, 0:2048]
        else:
            xv = np.ascontiguousarray(x[b])
        m["xv"] = xv
        maps.append(m)
    return maps


_NC_CACHE = {}


def kernel(**inputs):
    if "nc" not in _NC_CACHE:
        _NC_CACHE["nc"] = build_program()
    nc = _NC_CACHE["nc"]
    maps = make_in_maps(inputs)
    res = run_bass_kernel_spmd(nc, maps, core_ids=list(range(8)))
    out = np.zeros((4, 4096, D), dtype=np.float32)
    for core in range(8):
        b, half = core // 2, core % 2
        out[b, half * 2048:(half + 1) * 2048] = np.asarray(res.results[core]["out"])
    return out
```

```python
import numpy as np
import ml_dtypes
import concourse.bass as bass
import concourse.mybir as mybir
from concourse.ap import AP
from concourse.bass_utils import run_bass_kernel_spmd

F32 = mybir.dt.float32
BF16 = mybir.dt.bfloat16
AF = mybir.ActivationFunctionType
ALU = mybir.AluOpType
AX = mybir.AxisListType


class Prog:
    ENGS = ('pe', 'act', 'dve', 'pool', 'sp')

    def __init__(self, nc, n_dma_sems=48):
        self.nc = nc
        self.q = {e: [] for e in self.ENGS}
        self.sem = {e: nc.alloc_semaphore("sem_" + e) for e in self.ENGS}
        self.cnt = {e: 0 for e in self.ENGS}
        self.seen = {e: {} for e in self.ENGS}
        self.dsems = {'sp': [[nc.alloc_semaphore("dsp%d" % i), 0] for i in range(n_dma_sems)],
                      'pool': [[nc.alloc_semaphore("dpl%d" % i), 0] for i in range(28)],
                      'act': [[nc.alloc_semaphore("dac%d" % i), 0] for i in range(4)]}
        self.dnext = {'sp': 0, 'pool': 0, 'act': 0}
        self.last_w = {}
        self.readers = {}
        self.finals = []
        self.semobj = {}

    def _deps(self, eng, reads, writes):
        waits = {}

        def need(ev):
            if ev is None:
                return
            sid, val, src = ev
            if eng == 'pe' and src == 'pe':
                return
            if self.seen[eng].get(sid, 0) >= val:
                return
            if waits.get(sid, 0) < val:
                waits[sid] = val

        for k in reads:
            need(self.last_w.get(k))
        for k in writes:
            need(self.last_w.get(k))
            for sid, (val, src) in self.readers.get(k, {}).items():
                need((sid, val, src))
        return waits

    def _commit(self, eng, waits, fn, ev, inc, reads, writes):
        for sid, val in waits.items():
            self.seen[eng][sid] = val
        self.q[eng].append((list(waits.items()), fn, ev, inc))
        for k in reads:
            d = self.readers.setdefault(k, {})
            if d.get(ev[0], (0, None))[0] < ev[1]:
                d[ev[0]] = (ev[1], ev[2])
        for k in writes:
            self.last_w[k] = ev
            self.readers[k] = {}

    def op(self, eng, fn, reads=(), writes=()):
        waits = self._deps(eng, reads, writes)
        self.cnt[eng] += 1
        sem = self.sem[eng]
        self.semobj[id(sem)] = sem
        ev = (id(sem), self.cnt[eng], eng)
        self._commit(eng, waits, fn, ev, 1, reads, writes)

    def dma(self, eng, out, in_, reads=(), writes=(), final=False, pace=(), **kw):
        waits = self._deps(eng, list(reads) + list(pace), writes)
        pool_ = self.dsems[eng]
        slot = pool_[self.dnext[eng]]
        self.dnext[eng] = (self.dnext[eng] + 1) % len(pool_)
        sem = slot[0]
        self.semobj[id(sem)] = sem
        if slot[1] > 0 and self.seen[eng].get(id(sem), 0) < slot[1]:
            waits[id(sem)] = max(waits.get(id(sem), 0), slot[1])
        slot[1] += 16
        ev = (id(sem), slot[1], 'dma')
        self._commit(eng, waits, lambda e: e.dma_start(out=out, in_=in_, **kw), ev, 16, reads, writes)
        if final:
            self.finals.append(ev)

    def barrier(self):
        targets = {}
        for e in self.ENGS:
            if self.cnt[e] > 0:
                targets[id(self.sem[e])] = self.cnt[e]
        for sem, val in [x for lst in self.dsems.values() for x in lst]:
            if val > 0:
                targets[id(sem)] = val
                self.semobj[id(sem)] = sem
        for e in self.ENGS:
            waits = []
            for sid, val in targets.items():
                if sid == id(self.sem[e]) and e == 'pe':
                    pass
                if self.seen[e].get(sid, 0) < val:
                    waits.append((sid, val))
                    self.seen[e][sid] = val
            self.q[e].append((waits, None, None, 0))
        self.last_w = {}
        self.readers = {}

    def emit(self):
        nc = self.nc
        q = self.q
        semobj = self.semobj

        def run(e, lst):
            for waits, fn, ev, inc in lst:
                for sid, val in waits:
                    e.wait_ge(semobj[sid], val)
                if fn is not None:
                    ins = fn(e)
                    ins.then_inc(semobj[ev[0]], inc)

        with nc.Block() as block:
            @block.sync
            def _(e):
                run(e, q['sp'])

            @block.tensor
            def _(e):
                run(e, q['pe'])

            @block.scalar
            def _(e):
                run(e, q['act'])

            @block.vector
            def _(e):
                run(e, q['dve'])

            @block.gpsimd
            def _(e):
                run(e, q['pool'])
        self.q = {e: [] for e in self.ENGS}

    def finish(self):
        fw = [(sid, val) for sid, val, _ in self.finals]
        self.q['sp'].append((fw, None, None, 0))
        self.emit()

D = 2048
KC = 16
TV = 4096
TT = 256
NTILE = TV // TT
OWN0 = 2048
IN_COLS = 4656
DFF = 5632
NFC = 44
NEG = -30000.0
C_Q, C_KC, C_VC, C_KS, C_VS, C_KW, C_VW, C_GL, C_U, C_V = 0, 1024, 1280, 1536, 1792, 2048, 2304, 2560, 2608, 3632
GELU_K = 1.5957691216057308


def tap(t, p0, npart, off, dims):
    ps = t[:].ap[0][0]
    return AP(t, p0 * ps + off, [[ps, npart]] + [list(d) for d in dims])


def build_program(stop_after_mixer=False, dbg=False):
    nc = bass.Bass("TRN2", target_bir_lowering=False)

    def din(name, shape, dt=F32):
        return nc.dram_tensor(name, list(shape), dt, kind="ExternalInput").ap()

    xv = din("xv", [TV, D])
    w_in = din("w_in", [D, IN_COLS])
    w_out = din("w_out", [D, D])
    w_up = din("w_up", [D, 2 * DFF])
    w_down = din("w_down", [DFF, D])
    nw1_d = din("nw1", [128, KC])
    nw2_d = din("nw2", [128, KC])
    qw_d = din("qw", [64, 1])
    kw_d = din("kw", [64, 3])
    cpos_d = din("cpos", [128, 2, 16])
    cw1_d = din("cw1", [2, 2048, 256])
    cw2_d = din("cw2", [2, 256, 64])
    lnw_d = din("lnw", [128, 8])
    lnb_d = din("lnb", [128, 8])
    swT_d = din("swT", [128, 8, 128])
    sbb_d = din("sbb", [128, 8, 128])
    cw_d = din("cw", [128, 3, 88])
    cb_d = din("cb", [128, 88])
    ident_d = din("ident", [128, 128], BF16)
    qaug_d = din("qaug", [6, 16, TV], BF16)
    kaugs_d = din("kaugs", [6, TV], BF16)
    kaugc_d = din("kaugc", [6, 256], BF16)
    maskc_d = din("maskc", [128, 19, 128], BF16)
    cmask_d = din("cmask", [128, 2, 128], BF16)
    EB_d = din("EB", [96, TV], BF16)
    OVL_d = din("OVL", [128, 2, 65], BF16)
    fb_d = din("fb", [128, 17, 64], BF16)
    vval_d = din("vval", [128, 32], BF16)
    vcval_d = din("vcval", [128, 2])
    caus_d = din("caus", [128, 128])
    ones_d = din("ones64", [96, 64], BF16)

    out_d = nc.dram_tensor("out", [2048, D], F32, kind="ExternalOutput").ap()
    win_g = nc.dram_tensor("win_g", [19, 128, KC * 256], BF16).ap()
    wout_g = nc.dram_tensor("wout_g", [8, 128, KC * 256], BF16).ap()
    wup_g = nc.dram_tensor("wup_g", [2, 22, 128, KC * 256], BF16).ap()
    wdn_g = nc.dram_tensor("wdn_g", [4, 4, 128, 11 * 512], BF16).ap()
    x1h_d = nc.dram_tensor("x1h", [128, D], F32).ap()
    dbg_d = None
    if dbg:
        dbg_d = nc.dram_tensor("dbg", [128, 17, 2048], F32, kind="ExternalOutput").ap()

    P = Prog(nc)

    psA = [nc.alloc_psum_tensor("psA%d" % i, [128, 512], F32) for i in range(2)]
    psB = [nc.alloc_psum_tensor("psB%d" % i, [128, 512], F32) for i in range(2)]
    psC = nc.alloc_psum_tensor("psC", [128, 512], F32)
    psI = nc.alloc_psum_tensor("psI", [128, 512], F32)
    psS = nc.alloc_psum_tensor("psS", [128, 512], F32)
    psW = nc.alloc_psum_tensor("psW", [128, 512], F32)
    cntA = [0]
    cntB = [0]
    scnt = [0]

    def nextA():
        cntA[0] += 1
        i = cntA[0] % 2
        return psA[i], 'A%d' % i

    def nextB():
        cntB[0] += 1
        i = cntB[0] % 2
        return psB[i], 'B%d' % i

    win_groups = [(C_KC, 256), (C_VC, 256), (C_KS, 256), (C_KW, 256), (C_VS, 256), (C_VW, 256)]
    win_groups += [(C_Q + j * 256, 256) for j in range(4)] + [(C_GL, 48)]
    win_groups += [(C_U + j * 256, 256) for j in range(4)] + [(C_V + j * 256, 256) for j in range(4)]
    win_gid = {c0_: gi for gi, (c0_, n_) in enumerate(win_groups)}
    assert len(win_groups) == 19

    def win_src(c0_, n_):
        return win_g[win_gid[c0_]].rearrange("p (k c) -> p k c", c=256)[:, :, 0:n_]

    def wout_src(j):
        return wout_g[j].rearrange("p (k c) -> p k c", c=256)

    deferred_casts = []
    for gi_, (c0_, n_) in enumerate(win_groups):
        item_ = (win_src(c0_, n_), w_in[:, c0_:c0_ + n_].rearrange("(k p) c -> p k c", p=128), ('win', c0_))
        if gi_ < 6:
            P.dma('pool', item_[0], item_[1], writes=[item_[2]])
        else:
            deferred_casts.append(item_)
    for j in range(8):
        deferred_casts.append((wout_src(j), w_out[:, j * 256:(j + 1) * 256].rearrange("(k p) c -> p k c", p=128), ('wout', j)))
    for cg_ in range(22):
        for hf_ in range(2):
            c0_ = hf_ * DFF + cg_ * 256
            deferred_casts.append((wup_g[hf_, cg_].rearrange("p (k c) -> p k c", c=256),
                                   w_up[:, c0_:c0_ + 256].rearrange("(k p) c -> p k c", p=128), ('wup', hf_, cg_)))
    for ng_ in range(4):
        for s_ in range(4):
            deferred_casts.append((wdn_g[ng_, s_].rearrange("p (k c) -> p k c", c=512),
                                   w_down[s_ * 1408:(s_ + 1) * 1408, ng_ * 512:(ng_ + 1) * 512].rearrange("(k p) c -> p k c", p=128),
                                   ('wdn', ng_, s_)))

    def issue_casts(n, pace=()):
        for _ in range(n):
            if deferred_casts:
                o_, i_, k_ = deferred_casts.pop(0)
                P.dma('pool', o_, i_, writes=[k_], pace=pace)

    def mm_group(out, pairs, reads, writes):
        n = len(pairs)
        for i, (l, r) in enumerate(pairs):
            P.op('pe', lambda e, l=l, r=r, i=i: e.matmul(out, l, r, start=(i == 0), stop=(i == n - 1)),
                 reads=reads, writes=writes)

    def stageA1(halves, xb, ss, rs, tag):
        for i, (hap, hkey, c0, n) in enumerate(halves):
            P.op('act', lambda e, hap=hap, c0=c0, n=n, i=i: e.activation(xb[:, c0:c0 + n], hap, AF.Square, accum_out=ss[:, i:i + 1]),
                 reads=[hkey], writes=['xb' + tag, 'ss' + tag])
        if len(halves) == 2:
            P.op('dve', lambda e: e.tensor_tensor(ss[:, 0:1], ss[:, 0:1], ss[:, 1:2], ALU.add), reads=['ss' + tag], writes=['ss' + tag])
        P.op('act', lambda e: e.activation(rs[:], ss[:, 0:1], AF.Sqrt, scale=1.0 / D, bias=1e-6),
             reads=['ss' + tag], writes=['rs' + tag])
        P.op('dve', lambda e: e.reciprocal(rs[:], rs[:]), reads=['rs' + tag], writes=['rs' + tag])
        for hap, hkey, c0, n in halves:
            P.op('dve', lambda e, hap=hap, c0=c0, n=n: e.tensor_scalar(xb[:, c0:c0 + n], hap, rs[:, 0:1], None, ALU.mult),
                 reads=[hkey, 'rs' + tag], writes=['xb' + tag])

    def stageA2(xb, nw, ident, dstT, dcol0, dkey, tag):
        for h in range(2):
            ps, pk = nextA()
            psb = ps[:].bitcast(BF16)
            for k in range(8):
                kc = h * 8 + k
                P.op('pe', lambda e, kc=kc, k=k, psb=psb: e.transpose(psb[:, k * 128:(k + 1) * 128],
                                                                       xb[:, kc * 128:(kc + 1) * 128], ident[:]),
                     reads=['xb' + tag, 'ident'], writes=[pk])
            src = psb.rearrange("p (a b) -> p a b", b=128)
            nwb = tap(nw, 0, 128, h * 8, [[1, 8], [0, 128]])
            dst = dstT[:, h * 8:(h + 1) * 8, dcol0:dcol0 + 128]
            P.op('dve', lambda e, dst=dst, src=src, nwb=nwb: e.tensor_tensor(dst, src, nwb, ALU.mult),
                 reads=[pk, 'nw'], writes=[dkey])

    def rmsnorm_T(xblk, xkey, xb, junk, ss, rs, nw, ident, dstT, dcol0, dkey, tag):
        stageA1([(xblk[:], xkey, 0, D)], xb, ss, rs, tag)
        stageA2(xb, nw, ident, dstT, dcol0, dkey, tag)

    def gelu_ops(dst, t1, t2, k1, k2, kd, eng2='pool'):
        P.op(eng2, lambda e: e.tensor_tensor(t2, t1, t1, ALU.mult), reads=[k1], writes=[k2])
        P.op(eng2, lambda e: e.tensor_scalar(t2, t2, 0.044715, 1.0, ALU.mult, ALU.add), reads=[k2], writes=[k2])
        P.op(eng2, lambda e: e.tensor_tensor(t2, t2, t1, ALU.mult), reads=[k1, k2], writes=[k2])
        P.op('act', lambda e: e.activation(t2, t2, AF.Sigmoid, scale=GELU_K), reads=[k2], writes=[k2])
        P.op('dve', lambda e: e.tensor_tensor(dst, t1, t2, ALU.mult), reads=[k1, k2], writes=[kd])

    from contextlib import ExitStack
    with ExitStack() as es:
        def sb(name, shape, dt):
            return es.enter_context(nc.sbuf_tensor("s_" + name, list(shape), dt))

        KsT = sb("KsT", [96, 4, TV], BF16)
        KwT = sb("KwT", [96, 4, 1024], BF16)
        Vs = sb("Vs", [128, 32, 4, 65], BF16)
        Vw = sb("Vw", [128, 8, 4, 65], BF16)
        KcT = sb("KcT", [96, 4, 256], BF16)
        Vc = sb("Vc", [128, 2, 4, 65], BF16)
        h1v = sb("h1v", [128, 2, 4, 256], BF16)
        ident = sb("ident", [128, 128], BF16)
        maskc = sb("maskc", [128, 19, 128], BF16)
        cmask = sb("cmask", [128, 2, 128], BF16)
        EB = sb("EB", [96, TV], BF16)
        OVL = sb("OVL", [128, 2, 65], BF16)
        fb = sb("fb", [128, 17, 64], BF16)
        lnwT = sb("lnwT", [128, 8], F32)
        lnbT = sb("lnbT", [128, 8], F32)
        ones128 = sb("ones128", [128, 128], BF16)
        sbb = sb("sbb", [128, 8, 128], F32)
        swT = sb("swT", [128, 8, 128], BF16)
        nw1 = sb("nw1", [128, KC], F32)
        qw8 = sb("qw8", [64, 1], F32)
        kw3 = sb("kw3", [64, 3], F32)
        ones64 = sb("ones64", [96, 64], BF16)
        w1c = sb("w1c", [128, 2, 16, 256], BF16)
        w2c = sb("w2c", [128, 2, 2, 64], BF16)
        cposT = sb("cposT", [128, 2, 16], BF16)
        cbias = sb("cbias", [128, 4], F32)
        vval = sb("vval", [128, 32], BF16)
        vcval = sb("vcval", [128, 2], F32)
        gates = sb("gates", [128, 2, 48], F32)
        xblk = [sb("xblk%d" % i, [128, D], F32) for i in range(2)]
        xb = sb("xb", [128, D], BF16)
        junk = None
        ss = sb("ss", [128, 2], F32)
        rs = sb("rs", [128, 1], F32)
        xnTs = [sb("xnT%d" % i, [128, KC, TT], BF16) for i in range(2)]
        wbuf = [sb("wbuf%d" % i, [128, KC, 256], BF16) for i in range(2)]
        QT = sb("QT", [96, 16, TT], BF16)
        mixT = sb("mixT", [128, KC, TT], BF16)
        rawT2 = sb("rawT2", [128, 2, 4, 272], BF16)
        sq = sb("sq", [96, 512], BF16)
        rq = sb("rq", [64, 512], F32)
        vtm = [sb("vtm0", [128, 1024], F32)] * 2
        gt2 = sb("gt2", [128, 1024], F32)
        vn = sb("vn", [128, 1024], BF16)
        uT = sb("uT", [128, 8, TT], BF16)
        ut1 = sb("ut1", [128, TT], F32)
        ct1 = ut1[:].rearrange("p (a b) -> p a b", b=64)
        tmpo = ct1
        ct2 = gt2[:, 0:256].rearrange("p (a b) -> p a b", b=64)
        bnst = sb("bnst", [128, 2, 6], F32)
        bnag = sb("bnag", [128, 2], F32)
        lrs = sb("lrs", [128, 1], F32)
        PT = [sb("PT%d" % i, [128, 512], BF16) for i in range(4)]
        acc = [sb("acc%d" % i, [128, 4, 64], F32) for i in range(2)] * 2
        ablk = sb("ablk", [128, 1024], BF16)
        small = sb("small", [128, 64], F32)
        imp = sb("imp", [128, 64], F32)
        sc2 = sb("sc2", [128, 64], F32)
        m8 = sb("m8", [128, 16], F32)
        thr = sb("thr", [128, 1], F32)
        selbs = [sb("selb%d" % i, [128, 64], BF16) for i in range(2)]
        selT = [sb("selT%d" % i, [96, 128], BF16) for i in range(2)]
        h1 = sb("h1", [128, 4, 64], BF16)
        caus = None
        w1st = None

        def ld(dst, src, key, eng='sp'):
            P.dma(eng, dst, src, writes=[key])

        ld(ident[:], ident_d, 'ident')
        ld(nw1[:], nw1_d, 'nw')
        ld(qw8[:], qw_d, 'qw8')
        ld(kw3[:], kw_d, 'kw3')
        ld(ones64[:], ones_d, 'ones64')
        ld(maskc[:], maskc_d, 'maskc')
        ld(cmask[:], cmask_d, 'cmask')
        ld(EB[:], EB_d, 'EB')
        ld(OVL[:], OVL_d, 'OVL')
        ld(fb[:], fb_d, 'fb')
        ld(lnwT[:], lnw_d, 'lnwT')
        ld(lnbT[:], lnb_d, 'lnbT')
        ld(sbb[:], sbb_d, 'sbb')
        swf = gt2[:].rearrange("p (a b) -> p a b", b=128)
        ld(swf, swT_d, 'gt2')
        caus_ap = ut1[:, 0:128]
        ld(caus_ap, caus_d, 'ut1')
        ld(vval[:], vval_d, 'vval')
        ld(vcval[:], vcval_d, 'vcval')
        P.op('pool', lambda e: e.memset(KsT[64:96, :, :], 0.0), writes=['KsT'])
        P.op('pool', lambda e: e.memset(KwT[64:96, :, :], 0.0), writes=['KwT'])
        P.op('pool', lambda e: e.memset(KcT[64:96, :, :], 0.0), writes=['KcT'])
        P.op('pool', lambda e: e.memset(QT[64:96, :, :], 0.0), writes=['QT'])
        P.op('pool', lambda e: e.memset(sq[64:96, :], 0.0), writes=['sq'])
        for i_ in range(2):
            P.op('pool', lambda e, i_=i_: e.memset(selT[i_][64:96, :], 0.0), writes=['selT%d' % i_])
        for g in range(4):
            ld(KsT[64:70, g, :], kaugs_d, 'KsT')
            ld(KcT[64:70, g, :], kaugc_d, 'KcT')
        for kv in range(2):
            P.dma('pool', w1c[:, kv, :, :], cw1_d[kv].rearrange("(lp q) h -> q lp h", q=128), writes=['w1c'])
            P.dma('pool', w2c[:, kv, :, :], cw2_d[kv].rearrange("(hc q) d -> q hc d", q=128), writes=['w2c'])
        P.dma('pool', cposT[:], cpos_d, writes=['cposT'])
        P.op('dve', lambda e: e.tensor_scalar(qw8[:], qw8[:], 0.125, None, ALU.mult), reads=['qw8'], writes=['qw8'])
        cb_ap = tap(ut1, 0, 128, 0, [[0, 8], [1, 128]])
        P.op('dve', lambda e: e.tensor_tensor(swT[:], swf, cb_ap, ALU.mult), reads=['gt2', 'ut1'], writes=['swT'])
        P.op('pool', lambda e: e.memset(ones128[:], 1.0), writes=['ones128'])
        for hf_ in range(2):
            psw, pkw = nextA()
            P.op('pe', lambda e, psw=psw, hf_=hf_: e.matmul(psw[:], ones128[:], swT[:, hf_ * 4:(hf_ + 1) * 4, :], start=True, stop=True),
                 reads=['ones128', 'swT'], writes=[pkw])
            for gg_ in range(4):
                g_ = hf_ * 4 + gg_
                P.op('dve', lambda e, psw=psw, gg_=gg_, g_=g_: e.scalar_tensor_tensor(sbb[:, g_, :], psw[:, gg_ * 128:(gg_ + 1) * 128],
                                                                                  lnbT[:, g_:g_ + 1], sbb[:, g_, :], ALU.mult, ALU.add),
                     reads=[pkw, 'lnbT', 'sbb'], writes=['sbb'])
        P.op('pool', lambda e: e.memset(h1v[:], 0.0), writes=['h1v'])
        P.op('pool', lambda e: e.memset(KcT[0:64, :, :], 0.0), writes=['KcT'])
        P.op('pool', lambda e: e.memset(rawT2[:], 0.0), writes=['rawT2'])
        P.op('pool', lambda e: e.memset(Vc[:], 0.0), writes=['Vc'])
        for g in range(4):
            P.op('dve', lambda e, g=g: e.tensor_copy(Vs[:, :, g, 64], vval[:]), reads=['vval'], writes=['Vs'])
            P.op('dve', lambda e, g=g: e.tensor_copy(Vc[:, :, g, 64], vcval[:]), reads=['vcval', 'Vc'], writes=['Vc'])
        psb0, pkb0 = nextB()
        for kv in range(2):
            for hc in range(2):
                mm_group(psb0[:, (kv * 2 + hc):(kv * 2 + hc) + 1],
                         [(w1c[:, kv, lp, hc * 128:(hc + 1) * 128], cposT[:, kv, lp:lp + 1]) for lp in range(16)],
                         ['w1c', 'cposT'], [pkb0])
        P.op('act', lambda e: e.activation(cbias[:], psb0[:, 0:4], AF.Copy), reads=[pkb0], writes=['cbias'])

        wcnt = [0]
        cast_on = [0]

        def load_w(src_ap, ncols, keys):
            keys = [keys] if isinstance(keys, tuple) else keys
            i = wcnt[0] % 2
            wcnt[0] += 1
            key = 'wbuf%d' % i
            P.dma('sp', wbuf[i][:, :, 0:ncols], src_ap, reads=keys, writes=[key])
            if cast_on[0] == 2 or (cast_on[0] == 1 and wcnt[0] % 2 == 0):
                issue_casts(1, pace=[key])
            return wbuf[i], key

        def norm_fm(ps, pk, nsl, wcol, dst, dkey, extra_scale_key):
            n = nsl * 256
            P.op('act', lambda e: e.activation(sq[0:64, 0:n], ps[0:64, 0:n], AF.Square), reads=[pk], writes=['sq'])
            ps2, pk2 = nextB()
            P.op('pe', lambda e: e.matmul(ps2[0:64, 0:n], ones64[0:96, :], sq[0:96, 0:n], start=True, stop=True),
                 reads=['sq', 'ones64'], writes=[pk2])
            P.op('act', lambda e: e.activation(rq[:, 0:n], ps2[0:64, 0:n], AF.Ln, scale=1.0 / 64, bias=1e-6),
                 reads=[pk2], writes=['rq'])
            P.op('act', lambda e: e.activation(rq[:, 0:n], rq[:, 0:n], AF.Exp, scale=-0.5), reads=['rq'], writes=['rq'])
            src = ps[0:64, 0:n].rearrange("p (a b) -> p a b", b=256)
            rqv = rq[:, 0:n].rearrange("p (a b) -> p a b", b=256)
            P.op('dve', lambda e: e.scalar_tensor_tensor(dst, src, wcol, rqv, ALU.mult, ALU.mult),
                 reads=[pk, 'rq', extra_scale_key], writes=[dkey])

        def attention_block(qb, b):
            qi = qb - 15
            c0 = b * 128
            items = []
            order = ['c0', 'w0', 'c1', 's0', 'w1', 'c2', 's1', 'w2', 'c3', 's2', 'w3', 's3']
            nts = [0] if qb == 15 else [0, 1]
            for o in order:
                g = int(o[1])
                if o[0] == 'c':
                    for nt in nts:
                        items.append(('c', g, nt, nt == nts[0], nt == nts[-1]))
                elif o[0] == 'w':
                    kbs = list(range(qb - 4, qb + 1))
                    for kb in kbs:
                        items.append(('w', g, kb, kb == kbs[0], kb == kbs[-1]))
                else:
                    for kb in range(qb + 1):
                        items.append(('s', g, kb, kb == 0, kb == qb))

            def rhsQ(g):
                return tap(QT, 0, 96, (4 * g) * TT + c0, [[TT, 4], [1, 128]])

            st = {}

            def stage1(idx):
                kind, g, k, first, last = items[idx]
                if kind == 's' and first:
                    selection_finish(g)
                scnt[0] += 1
                ps, pk = [(psB[0], 'B0'), (psB[1], 'B1'), (psA[0], 'A0'), (psA[1], 'A1')][scnt[0] % 4]
                rq_ = rhsQ(g)
                if kind == 'c':
                    pairs = [(KcT[0:96, g, k * 128:(k + 1) * 128], rq_, ['KcT', 'QT'])]
                    mi = None
                    if k == 1:
                        mi = 2 + (qb - 16)
                    elif qb <= 16:
                        mi = qb - 15
                    else:
                        mi = 18
                    if mi is not None:
                        pairs.append((ident[:], tap(maskc, 0, 128, mi * 128, [[0, 4], [1, 128]]), ['ident', 'maskc']))
                elif kind == 'w':
                    slot = ((k // 2) % 4) * 256 + (k % 2) * 128
                    pairs = [(KwT[0:96, g, slot:slot + 128], rq_, ['KwT', 'QT'])]
                    d = qb - k
                    if d == 0:
                        pairs.append((ident[:], tap(cmask, 0, 128, 0, [[0, 4], [1, 128]]), ['ident', 'cmask']))
                    if d == 4:
                        pairs.append((ident[:], tap(cmask, 0, 128, 128, [[0, 4], [1, 128]]), ['ident', 'cmask']))
                else:
                    pairs = [(KsT[0:96, g, k * 128:(k + 1) * 128], rq_, ['KsT', 'QT']),
                             (EB[0:96, k * 128:(k + 1) * 128], tap(selT[g % 2], 0, 96, 0, [[0, 4], [1, 128]]),
                              ['EB', 'selT%d' % (g % 2)])]
                    if k == qb:
                        pairs.append((ident[:], tap(cmask, 0, 128, 0, [[0, 4], [1, 128]]), ['ident', 'cmask']))
                n = len(pairs)
                for i, (l, r, rd) in enumerate(pairs):
                    P.op('pe', lambda e, l=l, r=r, i=i, n=n, ps=ps: e.matmul(ps[:], l, r, start=(i == 0), stop=(i == n - 1)),
                         reads=rd, writes=[pk])
                pt = PT[idx % 4]
                ptk = 'PT%d' % (idx % 4)
                P.op('act', lambda e, pt=pt, ps=ps: e.activation(pt[:], ps[:], AF.Exp), reads=[pk], writes=[ptk])
                st[idx] = (pt, ptk)

            def stage3(idx):
                kind, g, k, first, last = items[idx]
                pt, ptk = st.pop(idx)
                if kind == 'c':
                    bI, kI, bC, kC = (psI, 'I', psC, 'C')
                    for r in range(4):
                        P.op('pe', lambda e, r=r, pt=pt, k=k, bI=bI: e.matmul(bI[:, r * 65:(r + 1) * 65], pt[:, r * 128:(r + 1) * 128],
                                                                              OVL[:, k, :], start=(first and r == 0), stop=(last and r == 3)),
                             reads=[ptk, 'OVL'], writes=[kI])
                    for r in range(4):
                        P.op('pe', lambda e, r=r, pt=pt, k=k, g=g, bC=bC: e.matmul(bC[:, r * 65:(r + 1) * 65], pt[:, r * 128:(r + 1) * 128],
                                                                                   Vc[:, k, g, :], start=(first and r == 0), stop=(last and r == 3)),
                             reads=[ptk, 'Vc'], writes=[kC])
                    if last:
                        selection(g, bI, kI)
                        fold(0, g, bC, kC)
                elif kind == 'w':
                    slot = k % 8
                    for r in range(4):
                        P.op('pe', lambda e, r=r, pt=pt, slot=slot, g=g: e.matmul(psW[:, r * 65:(r + 1) * 65], pt[:, r * 128:(r + 1) * 128],
                                                                                  Vw[:, slot, g, :], start=(first and r == 0), stop=(last and r == 3)),
                             reads=[ptk, 'Vw'], writes=['W'])
                    if last:
                        fold(2, g, psW, 'W')
                else:
                    for r in range(4):
                        P.op('pe', lambda e, r=r, pt=pt, k=k, g=g: e.matmul(psS[:, r * 65:(r + 1) * 65], pt[:, r * 128:(r + 1) * 128],
                                                                            Vs[:, k, g, :], start=(first and r == 0), stop=(last and r == 3)),
                             reads=[ptk, 'Vs'], writes=['S'])
                    if last:
                        fold(1, g, psS, 'S')

            def selection(g, psI, kI):
                den = tap(psI, 0, 128, 64, [[65, 4]])
                rdn = small[:, 0:4]
                P.op('dve', lambda e: e.tensor_scalar(rdn, den, 1e-30, None, ALU.max), reads=[kI], writes=['small'])
                P.op('dve', lambda e: e.reciprocal(rdn, rdn), reads=['small'], writes=['small'])
                P.op('dve', lambda e: e.tensor_scalar(imp[:], psI[:, 0:64], small[:, 0:1], None, ALU.mult),
                     reads=[kI, 'small'], writes=['imp'])
                for r in range(1, 4):
                    P.op('dve', lambda e, r=r: e.scalar_tensor_tensor(imp[:], psI[:, r * 65:r * 65 + 64], small[:, r:r + 1], imp[:],
                                                                      ALU.mult, ALU.add),
                         reads=[kI, 'small', 'imp'], writes=['imp'])
                P.op('dve', lambda e: e.tensor_tensor(imp[:], imp[:], fb[:, qi, :], ALU.add), reads=['imp', 'fb'], writes=['imp'])
                P.op('dve', lambda e: e.max(m8[:, 0:8], imp[:]), reads=['imp'], writes=['m8'])
                P.op('dve', lambda e: e.match_replace(sc2[:], m8[:, 0:8], imp[:], -3e9), reads=['imp', 'm8'], writes=['sc2'])
                P.op('dve', lambda e: e.max(m8[:, 8:16], sc2[:]), reads=['sc2'], writes=['m8'])
                P.op('dve', lambda e: e.tensor_scalar(thr[:], m8[:, 15:16], -0.5e9, None, ALU.max), reads=['m8'], writes=['thr'])
                sb_ = selbs[g % 2]
                P.op('dve', lambda e: e.tensor_scalar(sb_[:], imp[:], thr[:, 0:1], NEG, ALU.is_lt, ALU.mult),
                     reads=['imp', 'thr'], writes=['selb%d' % (g % 2)])

            def selection_finish(g):
                ps, pk = nextB()
                psb = ps[:].bitcast(BF16)
                sb_ = selbs[g % 2]
                P.op('pe', lambda e: e.transpose(psb[0:64, 0:128], sb_[:], ident[:]), reads=['selb%d' % (g % 2), 'ident'], writes=[pk])
                sT = selT[g % 2]
                P.op('act', lambda e: e.activation(sT[0:64, :], psb[0:64, 0:128], AF.Copy), reads=[pk], writes=['selT%d' % (g % 2)])

            def fold(br, g, ps, pk):
                o = 8 + br * 8
                rd = small[:, o:o + 4]
                gw = small[:, o + 4:o + 8]
                den = tap(ps, 0, 128, 64, [[65, 4]])
                P.op('dve', lambda e: e.tensor_scalar(rd, den, 1e-30, None, ALU.max), reads=[pk], writes=['small'])
                P.op('dve', lambda e: e.reciprocal(rd, rd), reads=['small'], writes=['small'])
                gsel = tap(gates, 0, 128, b * 48 + g * 12 + br, [[3, 4]])
                P.op('dve', lambda e: e.tensor_tensor(gw, rd, gsel, ALU.mult), reads=['small', 'gates'], writes=['small'])
                src = tap(ps, 0, 128, 0, [[65, 4], [1, 64]])
                gwb = tap(small, 0, 128, o + 4, [[1, 4], [0, 64]])
                akey = 'acc%d' % (g % 2)
                if br == 0:
                    P.op('dve', lambda e: e.tensor_tensor(acc[g][:], src, gwb, ALU.mult), reads=[pk, 'small'], writes=[akey])
                elif br == 2:
                    P.op('dve', lambda e: e.tensor_tensor(tmpo, src, gwb, ALU.mult), reads=[pk, 'small'], writes=['ut1'])
                    P.op('pool', lambda e: e.tensor_tensor(acc[g][:], acc[g][:], tmpo, ALU.add), reads=['ut1', akey], writes=[akey])
                else:
                    P.op('dve', lambda e: e.tensor_tensor(tmpo, src, gwb, ALU.mult), reads=[pk, 'small'], writes=['ut1'])
                    dst = ablk[:, g * 256:(g + 1) * 256].rearrange("p (a b) -> p a b", b=64)
                    P.op('pool', lambda e: e.tensor_tensor(dst, acc[g][:], tmpo, ALU.add), reads=['ut1', akey], writes=['ablk'])

            n = len(items)
            LAG = 2
            for idx in range(n + LAG):
                if idx < n:
                    stage1(idx)
                if idx >= LAG:
                    stage3(idx - LAG)
            if dbg_d is not None:
                P.dma('pool', dbg_d[:, qi, 0:1024], ablk[:], reads=['ablk'], writes=[('dbg', qi, 0)], final=True)
                P.dma('pool', dbg_d[:, qi, 1024:2048].rearrange("p (a b) -> p a b", b=128), mixT[:, 8:16, c0:c0 + 128],
                      reads=['mixT'], writes=[('dbg', qi, 1)], final=True)
            ps, pk = nextB()
            psb = ps[:].bitcast(BF16)
            for k in range(8):
                P.op('pe', lambda e, k=k: e.transpose(psb[:, k * 128:(k + 1) * 128], ablk[:, k * 128:(k + 1) * 128], ident[:]),
                     reads=['ablk', 'ident'], writes=[pk])
            P.op('act', lambda e: e.activation(mixT[:, 0:8, c0:c0 + 128], psb.rearrange("p (a b) -> p a b", b=128), AF.Copy),
                 reads=[pk], writes=['mixT'])

        def prefetch_A1(tn, b):
            blk = 2 * tn + b
            P.dma('sp', vtm[0][:], xv[blk * 128:(blk + 1) * 128, 0:1024], writes=['vtm'])
            P.dma('sp', gt2[:], xv[blk * 128:(blk + 1) * 128, 1024:2048], writes=['gt2'])
            stageA1([(vtm[0][:], 'vtm', 0, 1024), (gt2[:], 'gt2', 1024, 1024)], xb, ss, rs, '')

        def prefetch_A2(tn, b):
            stageA2(xb, nw1, ident, xnTs[tn % 2], b * 128, 'xnT%d' % (tn % 2), '')

        for b_ in range(2):
            prefetch_A1(0, b_)
            prefetch_A2(0, b_)
        for t in range(NTILE):
            own = t >= 8
            has_q = own or t == 7
            qblocks = [15] if t == 7 else ([2 * t, 2 * t + 1] if own else [])
            cast_on[0] = 1 if t >= 7 else 2
            xnT = xnTs[t % 2]
            XK = 'xnT%d' % (t % 2)
            if has_q:
                for b in ([1] if t == 7 else [0, 1]):
                    blk = 2 * t + b
                    P.dma('pool', xblk[b][:], xv[blk * 128:(blk + 1) * 128, :], writes=['xblk%d' % b])
            if has_q:
                P.dma('sp', QT[64:70, :, :], qaug_d[:, :, t * TT:(t + 1) * TT], writes=['QT'])
            slotw = (t % 4) * 256
            for g in range(4):
                P.dma('sp', KwT[64:70, g, slotw:slotw + 256], kaugs_d[:, t * TT:(t + 1) * TT], writes=['KwT'])
            rhs_all = [xnT[:, kc, :] for kc in range(KC)]
            if has_q:
                for j in range(4):
                    wb, wk = load_w(win_src(C_Q + j * 256, 256), 256, ('win', C_Q + j * 256))
                    for hp in range(2):
                        ps, pk = nextA()
                        for hh in range(2):
                            hl = 2 * hp + hh
                            mm_group(ps[0:64, hh * 256:(hh + 1) * 256],
                                     [(wb[:, kc, hl * 64:(hl + 1) * 64], rhs_all[kc]) for kc in range(KC)],
                                     [wk, XK], [pk])
                        h0 = 4 * j + 2 * hp
                        norm_fm(ps, pk, 2, qw8[:, 0:1], QT[0:64, h0:h0 + 2, :], 'QT', 'qw8')
            P.op('dve', lambda e: e.tensor_copy(rawT2[:, :, :, 0:16], rawT2[:, :, :, 256:272]), reads=['rawT2'], writes=['rawT2'])
            for kv in range(2):
                wb, wk = load_w(win_src(C_KC + kv * 256, 256), 256, ('win', C_KC + kv * 256))
                for gp in range(2):
                    ps, pk = nextA()
                    for gg in range(2):
                        g = 2 * gp + gg
                        mm_group(ps[0:64, gg * 256:(gg + 1) * 256],
                                 [(wb[:, kc, g * 64:(g + 1) * 64], rhs_all[kc]) for kc in range(KC)],
                                 [wk, XK], [pk])
                    P.op('act', lambda e, ps=ps, kv=kv, gp=gp: e.activation(
                        rawT2[0:64, kv, 2 * gp:2 * gp + 2, 16:272], ps[0:64, :].rearrange("p (a b) -> p a b", b=256), AF.Copy),
                        reads=[pk], writes=['rawT2'])
            P.dma('sp', rawT2[64:128, :, :, 15:271], rawT2[0:64, :, :, 16:272], reads=['rawT2'], writes=['rawT2'])
            for br, c_off in ((1, C_KS), (2, C_KW)):
                if br == 2 and t < 5:
                    continue
                wb, wk = load_w(win_src(c_off, 256), 256, ('win', c_off))
                for gp in range(2):
                    ps, pk = nextA()
                    for gg in range(2):
                        g = 2 * gp + gg
                        mm_group(ps[0:64, gg * 256:(gg + 1) * 256],
                                 [(wb[:, kc, g * 64:(g + 1) * 64], rhs_all[kc]) for kc in range(KC)],
                                 [wk, XK], [pk])
                    if br == 1:
                        dst = KsT[0:64, 2 * gp:2 * gp + 2, t * TT:(t + 1) * TT]
                        norm_fm(ps, pk, 2, kw3[:, 1:2], dst, 'KsT', 'kw3')
                    else:
                        dst = KwT[0:64, 2 * gp:2 * gp + 2, slotw:slotw + 256]
                        norm_fm(ps, pk, 2, kw3[:, 2:3], dst, 'KwT', 'kw3')
            for br, c_off in ((1, C_VS), (2, C_VW)):
                if br == 2 and t < 5:
                    continue
                wb, wk = load_w(win_src(c_off, 256), 256, ('win', c_off))
                for b in range(2):
                    blk = 2 * t + b
                    ps, pk = nextA()
                    mm_group(ps[:, 0:256], [(xnT[:, kc, b * 128:(b + 1) * 128], wb[:, kc, :]) for kc in range(KC)],
                             [wk, XK], [pk])
                    src = ps[:, 0:256].rearrange("p (a b) -> p a b", b=64)
                    if br == 1:
                        P.op('act', lambda e, src=src, blk=blk: e.activation(Vs[:, blk, :, 0:64], src, AF.Copy),
                             reads=[pk], writes=['Vs'])
                    else:
                        slot = blk % 8
                        P.op('act', lambda e, src=src, slot=slot: e.activation(Vw[:, slot, :, 0:64], src, AF.Copy),
                             reads=[pk], writes=['Vw'])
                        vb = tap(vval, 0, 128, blk, [[0, 4]])
                        P.op('dve', lambda e, slot=slot, vb=vb: e.tensor_copy(Vw[:, slot, :, 64], vb), reads=['vval'], writes=['Vw'])
            if not has_q and t + 1 < NTILE:
                for b_ in range(2):
                    prefetch_A1(t + 1, b_)
                    prefetch_A2(t + 1, b_)
            n0 = 16 * t - 1
            psc, pkc = nextA()
            for kv in range(2):
                for hc in range(2):
                    a = kv * 2 + hc
                    mm_group(psc[:, a * 64:(a + 1) * 64],
                             [(w1c[:, kv, lp, hc * 128:(hc + 1) * 128], tap(rawT2, 0, 128, kv * 4 * 272 + 2 * lp, [[272, 4], [16, 16]]))
                              for lp in range(16)], ['w1c', 'rawT2'], [pkc])
            for a in range(4):
                P.op('act', lambda e, a=a, psc=psc: e.activation(ct1[:, a, :], psc[:, a * 64:(a + 1) * 64], AF.Identity, bias=cbias[:, a:a + 1]),
                     reads=[pkc, 'cbias'], writes=['ut1'])
            gelu_ops(h1[:], ct1, ct2, 'ut1', 'gt2', 'h1')
            m0 = 1 if t == 0 else 0
            nm = 16 - m0
            psk, pkk = nextA()
            mm_group(psk[0:64, 0:64], [(w2c[:, 0, hc, :], h1[:, hc, :]) for hc in range(2)], ['w2c', 'h1'], [pkk])
            P.op('act', lambda e, psk=psk: e.activation(sq[0:64, 0:64], psk[0:64, 0:64], AF.Square), reads=[pkk], writes=['sq'])
            ps2, pk2 = nextB()
            P.op('pe', lambda e, ps2=ps2: e.matmul(ps2[0:64, 0:64], ones64[0:96, :], sq[0:96, 0:64], start=True, stop=True),
                 reads=['sq', 'ones64'], writes=[pk2])
            P.op('act', lambda e, ps2=ps2: e.activation(rq[:, 0:64], ps2[0:64, 0:64], AF.Ln, scale=1.0 / 64, bias=1e-6),
                 reads=[pk2], writes=['rq'])
            P.op('act', lambda e: e.activation(rq[:, 0:64], rq[:, 0:64], AF.Exp, scale=-0.5), reads=['rq'], writes=['rq'])
            srck = tap(psk, 0, 64, m0, [[16, 4], [1, nm]])
            rqk = tap(rq, 0, 64, m0, [[16, 4], [1, nm]])
            dstk = tap(KcT, 0, 64, n0 + m0, [[256, 4], [1, nm]])
            P.op('dve', lambda e, srck=srck, rqk=rqk, dstk=dstk: e.scalar_tensor_tensor(dstk, srck, kw3[:, 0:1], rqk, ALU.mult, ALU.mult),
                 reads=[pkk, 'rq', 'kw3'], writes=['KcT'])
            for hc in range(2):
                srcv = tap(h1, 0, 128, (2 + hc) * 64 + m0, [[16, 4], [1, nm]])
                dstv = tap(h1v, 0, 128, hc * 1024 + n0 + m0, [[256, 4], [1, nm]])
                P.op('pool', lambda e, srcv=srcv, dstv=dstv: e.tensor_copy(dstv, srcv), reads=['h1'], writes=['h1v'])
            for nt in sorted(set([max(n0, 0) // 128, (n0 + 15) // 128])):
                psv, pkv = nextA()
                for g in range(4):
                    mm_group(psv[:, g * 64:(g + 1) * 64],
                             [(h1v[:, hc, g, nt * 128:(nt + 1) * 128], w2c[:, 1, hc, :]) for hc in range(2)],
                             ['h1v', 'w2c'], [pkv])
                P.op('dve', lambda e, psv=psv, nt=nt: e.tensor_scalar(Vc[:, nt, :, 0:64], psv[:, 0:256].rearrange("p (a b) -> p a b", b=64),
                                                                     vcval[:, nt:nt + 1], None, ALU.mult),
                     reads=[pkv, 'vcval'], writes=['Vc'])
            if not has_q:
                continue
            blks = [1] if t == 7 else [0, 1]
            wb, wk = load_w(win_src(C_GL, 48), 48, ('win', C_GL))
            for b in blks:
                ps, pk = nextA()
                mm_group(ps[:, 0:48], [(xnT[:, kc, b * 128:(b + 1) * 128], wb[:, kc, 0:48]) for kc in range(KC)],
                         [wk, XK], [pk])
                P.op('act', lambda e, ps=ps, b=b: e.activation(gates[:, b, :], ps[:, 0:48], AF.Sigmoid), reads=[pk], writes=['gates'])
            for j in range(4):
                wb, wk = load_w(win_src(C_U + j * 256, 256), 256, ('win', C_U + j * 256))
                for cc in range(2):
                    ps, pk = nextA()
                    mm_group(ps[:, 0:256], [(wb[:, kc, cc * 128:(cc + 1) * 128], rhs_all[kc]) for kc in range(KC)],
                             [wk, XK], [pk])
                    P.op('act', lambda e, ps=ps: e.activation(ut1[:], ps[:, 0:256], AF.Copy), reads=[pk], writes=['ut1'])
                    gelu_ops(uT[:, 2 * j + cc, :], ut1[:], gt2[:, 0:256], 'ut1', 'gt2', 'uT')
            VBUF = [vtm[0], xb[:].bitcast(F32)]
            VKEY = ['vtm', 'xb']
            for j in range(4):
                wb, wk = load_w(win_src(C_V + j * 256, 256), 256, ('win', C_V + j * 256))
                for b in blks:
                    vb_ = VBUF[b][:, j * 256:(j + 1) * 256]
                    ps, pk = nextA()
                    mm_group(ps[:, 0:256], [(xnT[:, kc, b * 128:(b + 1) * 128], wb[:, kc, :]) for kc in range(KC)],
                             [wk, XK], [pk])
                    P.op('act', lambda e, ps=ps, vb_=vb_: e.activation(vb_, ps[:, 0:256], AF.Copy),
                         reads=[pk], writes=[VKEY[b], ('vtm', b, j), 'gt2'])
                    gelu_ops(vb_, vb_, gt2[:, j * 256:(j + 1) * 256], ('vtm', b, j), ('gt2', j), ('vtm', b, j))
            for b in blks:
                vk = VKEY[b]
                vt_ = VBUF[b]
                VP = [('vtm', b, j_) for j_ in range(4)]
                GP = [('gt2', j_) for j_ in range(4)]
                for c in range(2):
                    P.op('dve', lambda e, vt_=vt_, c=c: e.bn_stats(bnst[:, c, :], vt_[:, c * 512:(c + 1) * 512]), reads=[vk] + VP, writes=['bnst'])
                P.op('dve', lambda e: e.bn_aggr(bnag[:], bnst[:].rearrange("p a b -> p (a b)")), reads=['bnst'], writes=['bnag'])
                P.op('act', lambda e: e.activation(lrs[:], bnag[:, 1:2], AF.Sqrt, bias=1e-5), reads=['bnag'], writes=['lrs'])
                P.op('dve', lambda e: e.reciprocal(lrs[:], lrs[:]), reads=['lrs'], writes=['lrs'])
                P.op('dve', lambda e, vt_=vt_: e.tensor_scalar(vn[:], vt_[:], bnag[:, 0:1], lrs[:, 0:1], ALU.subtract, ALU.mult),
                     reads=[vk, 'bnag', 'lrs'] + VP + GP, writes=['vn', 'gt2', vk])
                attention_block(2 * t + b, b)
                for hf in range(2):
                    ps, pk = nextA()
                    for gg in range(4):
                        g = hf * 4 + gg
                        P.op('pe', lambda e, ps=ps, g=g, gg=gg: e.matmul(ps[:, gg * 128:(gg + 1) * 128], vn[:, g * 128:(g + 1) * 128],
                                                                         swT[:, g, :], start=True, stop=True),
                             reads=['vn', 'swT'], writes=[pk])
                    t3 = gt2[:, 0:512].rearrange("p (a b) -> p a b", b=128)
                    for gg in range(4):
                        g = hf * 4 + gg
                        P.op('dve', lambda e, ps=ps, gg=gg, g=g: e.scalar_tensor_tensor(gt2[:, gg * 128:(gg + 1) * 128], ps[:, gg * 128:(gg + 1) * 128],
                                                                                      lnwT[:, g:g + 1], sbb[:, g, :], ALU.mult, ALU.add),
                             reads=[pk, 'sbb', 'lnwT'], writes=['gt2'])
                    P.op('dve', lambda e, hf=hf, b=b, t3=t3: e.tensor_tensor(mixT[:, 8 + hf * 4:8 + (hf + 1) * 4, b * 128:(b + 1) * 128], t3,
                                                                             uT[:, hf * 4:(hf + 1) * 4, b * 128:(b + 1) * 128], ALU.mult),
                         reads=['gt2', 'uT'], writes=['mixT'])
            for j in range(8):
                if t + 1 < NTILE:
                    if j == 0:
                        prefetch_A1(t + 1, 0)
                    if j == 4:
                        prefetch_A2(t + 1, 0)
                        prefetch_A1(t + 1, 1)
                wb, wk = load_w(wout_src(j), 256, ('wout', j))
                for b in blks:
                    ps, pk = nextA()
                    mm_group(ps[:, 0:256], [(mixT[:, kc, b * 128:(b + 1) * 128], wb[:, kc, :]) for kc in range(KC)],
                             [wk, 'mixT'], [pk])
                    P.op('dve', lambda e, ps=ps, b=b, j=j: e.tensor_tensor(xblk[b][:, j * 256:(j + 1) * 256], xblk[b][:, j * 256:(j + 1) * 256],
                                                                           ps[:, 0:256], ALU.add),
                         reads=[pk, 'xblk%d' % b], writes=['xblk%d' % b])
            if t + 1 < NTILE:
                prefetch_A2(t + 1, 1)
            for b in blks:
                blk = 2 * t + b
                if blk == 15:
                    P.dma('pool', x1h_d, xblk[b][:], reads=['xblk%d' % b], writes=['x1h'])
                else:
                    r0 = (blk - 16) * 128
                    P.dma('pool', out_d[r0:r0 + 128, :], xblk[b][:], reads=['xblk%d' % b], writes=[('out', blk - 16)],
                          final=stop_after_mixer)
        issue_casts(1000)
        P.barrier()
        P.emit()

    if stop_after_mixer:
        P.finish()
        return nc

    with ExitStack() as es:
        def sb(name, shape, dt):
            return es.enter_context(nc.sbuf_tensor("s_" + name, list(shape), dt))

        ident = sb("ident2", [128, 128], BF16)
        nw2 = sb("nw2", [128, KC], F32)
        cw = sb("cw", [128, 3, 88], F32)
        cb = sb("cb", [128, 88], F32)
        hprev = sb("hprev", [128, 88, 2], F32)
        x1b = [sb("x1b%d" % i, [128, D], F32) for i in range(4)]
        xb = sb("xb2", [128, D], BF16)
        junk = None
        ss = sb("ss2", [128, 2], F32)
        rs = sb("rs2", [128, 1], F32)
        xn2T = sb("xn2T", [128, KC, 512], BF16)
        xnhT = sb("xnhT", [128, KC, 128], BF16)
        yT = sb("yT", [128, NFC, 512], BF16)
        wu = [sb("wu%d" % i, [128, 2, KC, 256], BF16) for i in range(2)]
        wd = [sb("wd%d" % i, [128, 11, 512], BF16) for i in range(2)]
        hb = [sb("hb%d" % i, [128, 514], F32) for i in range(2)]
        ac = [sb("ac%d" % i, [128, 512], F32) for i in range(2)]
        sg = sb("sg", [128, 512], F32)

        P.dma('sp', ident[:], ident_d, writes=['ident'])
        P.dma('sp', nw2[:], nw2_d, writes=['nw'])
        P.dma('sp', cw[:], cw_d, writes=['cw'])
        P.dma('sp', cb[:], cb_d, writes=['cb'])
        P.dma('sp', x1b[0][:], x1h_d, reads=['x1h'], writes=['x1b0'])
        rmsnorm_T(x1b[0], 'x1b0', xb, junk, ss, rs, nw2, ident, xnhT, 0, 'xnhT', '2')
        psD = [psC, psI, psS, psW]
        psDk = ['C', 'I', 'S', 'W']
        wucnt = [0]
        wdcnt = [0]
        for ft in range(4):
            for b in range(4):
                r0 = ft * 512 + b * 128
                P.dma('sp', x1b[b][:], out_d[r0:r0 + 128, :], reads=[('out', ft * 4 + b)], writes=['x1b%d' % b])
                rmsnorm_T(x1b[b], 'x1b%d' % b, xb, junk, ss, rs, nw2, ident, xn2T, b * 128, 'xn2T', '2')
            for cg in range(22):
                i = wucnt[0] % 2
                wucnt[0] += 1
                wuk = 'wu%d' % i
                for hf in range(2):
                    c_off = hf * DFF + cg * 256
                    P.dma('sp', wu[i][:, hf, :, :], wup_g[hf, cg].rearrange("p (k c) -> p k c", c=256),
                          reads=[('wup', hf, cg)], writes=[wuk])
                for cc in range(2):
                    c = 2 * cg + cc
                    accs = []
                    for hf in range(2):
                        ch = hf * NFC + c
                        if ft == 0:
                            psh, pkh = nextB()
                            mm_group(psh[:, 0:2], [(wu[i][:, hf, kc, cc * 128:(cc + 1) * 128], xnhT[:, kc, 126:128]) for kc in range(KC)],
                                     [wuk, 'xnhT'], [pkh])
                            P.op('act', lambda e, psh=psh, ch=ch: e.activation(hprev[:, ch, :], psh[:, 0:2], AF.Copy),
                                 reads=[pkh], writes=['hprev'])
                        ps, pk = nextA()
                        mm_group(ps[:], [(wu[i][:, hf, kc, cc * 128:(cc + 1) * 128], xn2T[:, kc, :]) for kc in range(KC)],
                                 [wuk, 'xn2T'], [pk])
                        h_ = hb[hf]
                        hk = 'hb%d' % hf
                        a_ = ac[hf]
                        ak = 'ac%d' % hf
                        P.op('pool', lambda e, h_=h_, ch=ch: e.tensor_copy(h_[:, 0:2], hprev[:, ch, :]), reads=['hprev'], writes=[hk])
                        P.op('act', lambda e, h_=h_, ps=ps: e.activation(h_[:, 2:514], ps[:], AF.Copy), reads=[pk], writes=[hk])
                        P.op('pool', lambda e, h_=h_, ch=ch: e.tensor_copy(hprev[:, ch, :], h_[:, 512:514]), reads=[hk], writes=['hprev'])
                        P.op('act', lambda e, a_=a_, ps=ps, ch=ch: e.activation(a_[:], ps[:], AF.Identity, bias=cb[:, ch:ch + 1],
                                                                              scale=cw[:, 2, ch:ch + 1]),
                             reads=[pk, 'cw', 'cb'], writes=[ak])
                        P.op('dve', lambda e, a_=a_, h_=h_, ch=ch: e.scalar_tensor_tensor(a_[:], h_[:, 1:513], cw[:, 1, ch:ch + 1], a_[:],
                                                                                       ALU.mult, ALU.add),
                             reads=[hk, ak, 'cw'], writes=[ak])
                        P.op('dve', lambda e, a_=a_, h_=h_, ch=ch: e.scalar_tensor_tensor(a_[:], h_[:, 0:512], cw[:, 0, ch:ch + 1], a_[:],
                                                                                       ALU.mult, ALU.add),
                             reads=[hk, ak, 'cw'], writes=[ak])
                    P.op('act', lambda e: e.activation(sg[:], ac[0][:], AF.Silu), reads=['ac0'], writes=['sg'])
                    P.op('dve', lambda e, c=c: e.tensor_tensor(yT[:, c, :], sg[:], ac[1][:], ALU.mult), reads=['sg', 'ac1'], writes=['yT'])
            for ng in range(4):
                for s in range(4):
                    i = wdcnt[0] % 2
                    wdcnt[0] += 1
                    wdk = 'wd%d' % i
                    P.dma('sp', wd[i][:], wdn_g[ng, s].rearrange("p (k c) -> p k c", c=512),
                          reads=[('wdn', ng, s)], writes=[wdk])
                    for b in range(4):
                        for k in range(11):
                            kk = s * 11 + k
                            P.op('pe', lambda e, b=b, k=k, kk=kk, i=i: e.matmul(psD[b][:], yT[:, kk, b * 128:(b + 1) * 128], wd[i][:, k, :],
                                                                                start=(kk == 0), stop=(kk == NFC - 1)),
                                 reads=['yT', wdk], writes=[psDk[b]])
                for b in range(4):
                    P.op('dve', lambda e, b=b, ng=ng: e.tensor_tensor(x1b[b][:, ng * 512:(ng + 1) * 512], x1b[b][:, ng * 512:(ng + 1) * 512],
                                                                     psD[b][:], ALU.add),
                         reads=[psDk[b], 'x1b%d' % b], writes=['x1b%d' % b])
            for b in range(4):
                r0 = ft * 512 + b * 128
                P.dma('pool', out_d[r0:r0 + 128, :], x1b[b][:], reads=['x1b%d' % b], writes=[('out', ft * 4 + b)], final=True)
        P.barrier()
        P.emit()
    P.finish()
    return nc


_bf = ml_dtypes.bfloat16


def _bf_split(v):
    hi = v.astype(np.float32).astype(_bf)
    lo = (v.astype(np.float64) - hi.astype(np.float64)).astype(np.float32).astype(_bf)
    return hi, lo


def _const_tables(half):
    c = {}
    slopes = np.power(2.0, -8.0 * np.arange(1, 17) / 16).astype(np.float32)
    s_hi, s_lo = _bf_split(slopes)
    tpos = np.arange(TV)
    cc = -(slopes.astype(np.float64)[:, None] * tpos[None, :])
    c_hi, c_lo = _bf_split(cc)
    qaug = np.zeros((6, 16, TV), dtype=_bf)
    qaug[0] = s_hi[:, None]
    qaug[1] = s_hi[:, None]
    qaug[2] = s_lo[:, None]
    qaug[3] = s_lo[:, None]
    qaug[4] = c_hi
    qaug[5] = c_lo
    c["qaug"] = qaug
    ka = np.zeros((6, TV), dtype=np.float32)
    ka[0] = (tpos // 64) * 64
    ka[1] = tpos % 64
    ka[2] = ka[0]
    ka[3] = ka[1]
    ka[4] = 1.0
    ka[5] = 1.0
    c["kaugs"] = ka.astype(_bf)
    n = np.arange(256)
    kc = np.zeros((6, 256), dtype=np.float32)
    kc[0] = 16 * n
    kc[1] = 15.5
    kc[2] = 16 * n
    kc[3] = 15.5
    kc[4] = 1.0
    kc[5] = 1.0
    c["kaugc"] = kc.astype(_bf)
    pats = [(15, 0), (16, 0)] + [(q_, 1) for q_ in range(16, 32)]
    mk = np.zeros((128, 19, 128), dtype=np.float32)
    for pi, (q_, nt_) in enumerate(pats):
        nn = (128 * nt_ + np.arange(128))[:, None]
        tt = (128 * q_ + np.arange(128))[None, :]
        okc = (16 * nn + 31 <= tt) & (nn <= 254) & ((half == 1) | (nn >= 128))
        mk[:, pi, :] = np.where(okc, 0.0, NEG)
    mk[:, 18, :] = 0.0 if half == 1 else NEG
    c["maskc"] = mk.astype(_bf)
    j = np.arange(128)[:, None]
    ii = np.arange(128)[None, :]
    cm = np.zeros((128, 2, 128), dtype=np.float32)
    cm[:, 0, :] = np.where(j <= ii, 0.0, NEG)
    cm[:, 1, :] = np.where(j > ii, 0.0, NEG)
    c["cmask"] = cm.astype(_bf)
    eb = np.zeros((96, TV), dtype=np.float32)
    eb[tpos // 64, tpos] = 1.0
    c["EB"] = eb.astype(_bf)
    ovl = np.zeros((128, 2, 65), dtype=np.float32)
    for t2 in range(2):
        nn2 = 128 * t2 + np.arange(128)
        start = (nn2 * 16)[:, None]
        s0 = (np.arange(64) * 64)[None, :]
        ov = ((start < s0 + 64) & (start + 32 > s0)).astype(np.float32)
        ok = (nn2 <= 254)
        ov = ov * ok[:, None]
        vc = ok & ((half == 1) | (nn2 >= 128))
        ovl[:, t2, 0:64] = ov
        ovl[:, t2, 64] = vc
    c["OVL"] = ovl.astype(_bf)
    c["vcval"] = np.ascontiguousarray(ovl[:, :, 64]).astype(np.float32)
    first = 0 if half == 1 else 32
    fbm = np.zeros((128, 17, 64), dtype=np.float32)
    for qi in range(17):
        t3 = 128 * (15 + qi) + np.arange(128)
        cur = (t3 // 64)[:, None]
        jj = np.arange(64)[None, :]
        valid = (jj <= cur) & (jj >= first)
        forced = (jj == first) | (jj == cur) | (jj == cur - 1)
        fbm[:, qi, :] = np.where(valid, np.where(forced, 1e9, 0.0), -1e9)
    c["fb"] = fbm.astype(_bf)
    vv = np.zeros((128, 32), dtype=np.float32)
    vv[:, :] = ((half == 1) | (np.arange(32) >= 16))[None, :]
    c["vval"] = vv.astype(_bf)
    c["caus"] = (j <= ii).astype(np.float32)
    o96 = np.zeros((96, 64), dtype=np.float32)
    o96[0:64] = 1.0
    c["ones64"] = o96.astype(_bf)
    c["ident"] = np.eye(128, dtype=np.float32).astype(_bf)
    return c


def _shared_inputs(inp):
    f = np.float32
    s = {}
    s["w_in"] = np.ascontiguousarray(inp["w_in"], dtype=f)
    s["w_out"] = np.ascontiguousarray(inp["w_out"], dtype=f)
    s["w_up"] = np.ascontiguousarray(inp["w_up"], dtype=f)
    s["w_down"] = np.ascontiguousarray(inp["w_down"], dtype=f)
    s["nw1"] = np.ascontiguousarray(np.asarray(inp["attn_norm_w"], dtype=f).reshape(KC, 128).T)
    s["nw2"] = np.ascontiguousarray(np.asarray(inp["ffn_norm_w"], dtype=f).reshape(KC, 128).T)
    s["qw"] = np.ascontiguousarray(np.asarray(inp["q_norm_w"], dtype=f).reshape(64, 1))
    s["kw"] = np.ascontiguousarray(np.asarray(inp["k_norm_w"], dtype=f).T)
    cp = np.asarray(inp["cmp_pos"], dtype=f)
    s["cpos"] = np.ascontiguousarray(cp.reshape(2, 16, 2, 64).transpose(2, 3, 0, 1).reshape(128, 2, 16))
    s["cw1"] = np.ascontiguousarray(inp["cmp_w1"], dtype=f)
    s["cw2"] = np.ascontiguousarray(inp["cmp_w2"], dtype=f)
    s["lnw"] = np.ascontiguousarray(np.asarray(inp["gmlp_ln_w"], dtype=f).reshape(8, 128).T)
    s["lnb"] = np.ascontiguousarray(np.asarray(inp["gmlp_ln_b"], dtype=f).reshape(8, 128).T)
    s["swT"] = np.ascontiguousarray(np.asarray(inp["spatial_w"], dtype=f).transpose(2, 0, 1))
    s["sbb"] = np.ascontiguousarray(np.broadcast_to(np.asarray(inp["spatial_b"], dtype=f)[None, :, :], (128, 8, 128)))
    s["cw"] = np.ascontiguousarray(np.asarray(inp["conv_w"], dtype=f).reshape(3, 88, 128).transpose(2, 0, 1))
    s["cb"] = np.ascontiguousarray(np.asarray(inp["conv_b"], dtype=f).reshape(88, 128).T)
    return s


def make_in_maps(inp):
    x = np.asarray(inp["x"], dtype=np.float32)
    shared = _shared_inputs(inp)
    consts = [_const_tables(0), _const_tables(1)]
    maps = []
    for core in range(8):
        b, half = core // 2, core % 2
        m = dict(shared)
        m.update(consts[half])
        if half == 0:
            xv = np.zeros((TV, D), dtype=np.float32)
            xv[OWN0:] = x[b, 0:2048]
        else:
            xv = np.ascontiguousarray(x[b])
        m["xv"] = xv
        maps.append(m)
    return maps


_NC_CACHE = {}


def kernel(**inputs):
    if "nc" not in _NC_CACHE:
        _NC_CACHE["nc"] = build_program()
    nc = _NC_CACHE["nc"]
    maps = make_in_maps(inputs)
    res = run_bass_kernel_spmd(nc, maps, core_ids=list(range(8)))
    out = np.zeros((4, 4096, D), dtype=np.float32)
    for core in range(8):
        b, half = core // 2, core % 2
        out[b, half * 2048:(half + 1) * 2048] = np.asarray(res.results[core]["out"])
    return out
```
